# Optimizing a Trainium2 kernel written in Bass

```python
import jax, jax.numpy as jnp
from jax import lax
import numpy as np

D_MODEL = 2048
BATCH = 4
SEQ = 2048
DEPTH = 4
DEC_BATCH = 128
DEC_SEQ = 1
PAST_LEN = 16384
PAGE_SIZE = 128

N_BRANCH = 3
BRANCH_WIDTH = D_MODEL // 2
SGU_GROUPS = 8
SGU_CHUNK = 128
SGU_GROUP_WIDTH = BRANCH_WIDTH // SGU_GROUPS
HG_HEADS = 8
HG_DK = BRANCH_WIDTH // HG_HEADS
HG_DV = BRANCH_WIDTH // HG_HEADS
GLA_HEADS = 4
GLA_DK_TOTAL = BRANCH_WIDTH // 2
GLA_DK = GLA_DK_TOTAL // GLA_HEADS
GLA_DV = BRANCH_WIDTH // GLA_HEADS
GLA_RANK = 16
GLA_TAU = 16.0
D_FF = 4 * D_MODEL
REC_CHUNK = 64
EPS = 1e-6
IN_SPLITS = (2 * BRANCH_WIDTH, BRANCH_WIDTH, BRANCH_WIDTH, BRANCH_WIDTH, BRANCH_WIDTH, GLA_DK_TOTAL, GLA_DK_TOTAL, BRANCH_WIDTH, BRANCH_WIDTH, GLA_RANK, N_BRANCH * D_MODEL)
IN_WIDTH = sum(IN_SPLITS)

kernel_name = 'hybrid_sgu_hgrn2_gla_decoder_step'


def rmsnorm(x, g):
    xf = x.astype(jnp.float32)
    y = xf * lax.rsqrt(jnp.mean(xf * xf, axis=-1, keepdims=True) + EPS)
    return (y * g.astype(jnp.float32)).astype(x.dtype)


def layernorm(x, g, b):
    xf = x.astype(jnp.float32)
    mu = jnp.mean(xf, axis=-1, keepdims=True)
    xc = xf - mu
    y = xc * lax.rsqrt(jnp.mean(xc * xc, axis=-1, keepdims=True) + EPS)
    return (y * g.astype(jnp.float32) + b.astype(jnp.float32)).astype(x.dtype)


def head_rmsnorm(o, g):
    y = o * lax.rsqrt(jnp.mean(o * o, axis=-1, keepdims=True) + EPS)
    return y * g.astype(jnp.float32)


def gated_linear_recurrence(q, k, v, log_f, s0):
    B, T, H, _ = q.shape
    dv = v.shape[-1]
    C = min(REC_CHUNK, T)
    n = -(-T // C)
    pad = n * C - T

    def prep(a):
        a = jnp.pad(a.astype(jnp.float32), ((0, 0), (0, pad), (0, 0), (0, 0)))
        return a.reshape(B, n, C, H, a.shape[-1]).transpose(1, 0, 2, 3, 4)

    qs, ks, vs, ls = prep(q), prep(k), prep(v), prep(log_f)
    causal = jnp.tril(jnp.ones((C, C), dtype=bool))[None, :, :, None, None]

    def step(S, inp):
        qc, kc, vc, lc = inp
        b = jnp.cumsum(lc, axis=1)
        diff = b[:, :, None] - b[:, None, :]
        decay = jnp.exp(jnp.where(causal, diff, -jnp.inf))
        att = jnp.einsum('bthd,bshd,btshd->bhts', qc, kc, decay)
        o = jnp.einsum('bhts,bshv->bthv', att, vc) + jnp.einsum('bthd,bhdv->bthv', qc * jnp.exp(b), S)
        b_last = b[:, -1]
        S = jnp.exp(b_last)[..., None] * S + jnp.einsum('bshd,bshv->bhdv', kc * jnp.exp(b_last[:, None] - b), vc)
        return S, o

    S, os_ = lax.scan(step, s0.astype(jnp.float32), (qs, ks, vs, ls))
    o = os_.transpose(1, 0, 2, 3, 4).reshape(B, n * C, H, dv)[:, :T]
    return o, S


def spatial_gating(u, v, w_s, b_s):
    B, T, G, e = v.shape
    n = -(-T // SGU_CHUNK)
    pad = n * SGU_CHUNK - T
    vp = jnp.pad(v, ((0, 0), (0, pad), (0, 0), (0, 0))).reshape(B, n, SGU_CHUNK, G, e)
    mask = jnp.tril(jnp.ones((SGU_CHUNK, SGU_CHUNK), dtype=v.dtype))
    s = jnp.einsum('gnm,bcmge->bcnge', w_s * mask, vp) + b_s.T[:, :, None]
    s = s.reshape(B, n * SGU_CHUNK, G, e)[:, :T]
    return u * s


def hgrn_lower_bounds(hg_lb):
    p = jax.nn.softmax(hg_lb.astype(jnp.float32), axis=0)
    return jnp.maximum(jnp.cumsum(p, axis=0) - p[0:1], 0.0)


def trunk(x, c, s_hg, s_gla, p):
    B, T, _ = x.shape
    lb_all = hgrn_lower_bounds(p['hg_lb'])
    hg_states, gla_states, v_rows = [], [], []
    offsets = [int(o) for o in np.cumsum(IN_SPLITS)[:-1]]
    for l in range(DEPTH):
        mod = jax.nn.silu(c) @ p['w_ada'][l] + p['b_ada'][l]
        sh1, sc1, gt1, sh2, sc2, gt2 = jnp.split(mod[:, None, :], 6, axis=-1)

        h = rmsnorm(x, p['g_pre_mix'][l]) * (1.0 + sc1) + sh1
        z = h @ p['w_in'][l]
        (z_uv, hq, hf, hi, hg, gq, gk, gv, gr, glr, z_gate) = jnp.split(z, offsets, axis=-1)

        uv = jax.nn.gelu(z_uv)
        u, v = jnp.split(uv, 2, axis=-1)
        v = layernorm(v, p['sgu_ln_g'][l], p['sgu_ln_b'][l])
        v_rows.append(v)
        y_a = spatial_gating(u.reshape(B, T, SGU_GROUPS, SGU_GROUP_WIDTH), v.reshape(B, T, SGU_GROUPS, SGU_GROUP_WIDTH), p['sgu_w_s'][l], p['sgu_b_s'][l]).reshape(B, T, BRANCH_WIDTH)

        lb = lb_all[l].reshape(HG_HEADS, HG_DK)
        zf = hf.astype(jnp.float32).reshape(B, T, HG_HEADS, HG_DK)
        log_f = jnp.logaddexp(jnp.log(lb), jnp.log1p(-lb) + jax.nn.log_sigmoid(zf))
        k_hg = (1.0 - lb) * jax.nn.sigmoid(-zf)
        q_hg = jax.nn.silu(hq.astype(jnp.float32)).reshape(B, T, HG_HEADS, HG_DK)
        o_hg, S_hg = gated_linear_recurrence(q_hg, k_hg, hi.reshape(B, T, HG_HEADS, HG_DV), log_f, s_hg[l])
        y_b = (head_rmsnorm(o_hg, p['hg_norm_g'][l]).reshape(B, T, BRANCH_WIDTH) * jax.nn.silu(hg.astype(jnp.float32))).astype(x.dtype)

        log_a = jax.nn.log_sigmoid((glr @ p['gla_w_up'][l] + p['gla_b_up'][l]).astype(jnp.float32)) / GLA_TAU
        q_g = gq.reshape(B, T, GLA_HEADS, GLA_DK)
        k_g = gk.reshape(B, T, GLA_HEADS, GLA_DK).astype(jnp.float32) * (GLA_DK ** -0.5)
        o_g, S_g = gated_linear_recurrence(q_g, k_g, gv.reshape(B, T, GLA_HEADS, GLA_DV), log_a.reshape(B, T, GLA_HEADS, GLA_DK), s_gla[l])
        y_c = (head_rmsnorm(o_g, p['gla_norm_g'][l]).reshape(B, T, BRANCH_WIDTH) * jax.nn.silu(gr.astype(jnp.float32))).astype(x.dtype)

        gates = jax.nn.sigmoid(z_gate).reshape(B, T, N_BRANCH, D_MODEL)
        branches = (y_a, y_b, y_c)
        merged = gates[:, :, 0] * (branches[0] @ p['w_branch'][l, 0])
        for n_b in range(1, N_BRANCH):
            merged = merged + gates[:, :, n_b] * (branches[n_b] @ p['w_branch'][l, n_b])
        out = merged @ p['w_out'][l]
        x = x + gt1 * rmsnorm(out, p['g_post_mix'][l])

        h2 = rmsnorm(x, p['g_pre_mlp'][l]) * (1.0 + sc2) + sh2
        y2 = jnp.square(jax.nn.relu(h2 @ p['w_mlp_up'][l])) @ p['w_mlp_down'][l]
        x = x + gt2 * rmsnorm(y2, p['g_post_mlp'][l])

        hg_states.append(S_hg)
        gla_states.append(S_g)
    return x, jnp.stack(hg_states), jnp.stack(gla_states), jnp.stack(v_rows)


def setup_inputs(seed: int = 0) -> dict:
    key = jax.random.key(seed)
    ks = jax.random.split(key, 32)

    def nrm(k, shape, scale):
        return jax.random.normal(k, shape, jnp.float32) * scale

    def gain(k, shape):
        return 1.0 + 0.1 * jax.random.normal(k, shape, jnp.float32)

    return {
        'x_prompt': nrm(ks[0], (BATCH, SEQ, D_MODEL), 1.0),
        'x_sample': nrm(ks[1], (DEC_BATCH, DEC_SEQ, D_MODEL), 1.0),
        'state_hgrn': nrm(ks[2], (DEPTH, DEC_BATCH, HG_HEADS, HG_DK, HG_DV), 0.5),
        'state_gla': nrm(ks[3], (DEPTH, DEC_BATCH, GLA_HEADS, GLA_DK, GLA_DV), 0.5),
        'c_prompt': nrm(ks[4], (BATCH, D_MODEL), 1.0),
        'c_sample': nrm(ks[5], (DEC_BATCH, D_MODEL), 1.0),
        'w_ada': nrm(ks[6], (DEPTH, D_MODEL, 6 * D_MODEL), 0.5 * D_MODEL ** -0.5),
        'b_ada': nrm(ks[7], (DEPTH, 6 * D_MODEL), 0.01),
        'g_pre_mix': gain(ks[8], (DEPTH, D_MODEL)),
        'g_post_mix': gain(ks[9], (DEPTH, D_MODEL)),
        'g_pre_mlp': gain(ks[10], (DEPTH, D_MODEL)),
        'g_post_mlp': gain(ks[11], (DEPTH, D_MODEL)),
        'w_in': nrm(ks[12], (DEPTH, D_MODEL, IN_WIDTH), D_MODEL ** -0.5),
        'sgu_ln_g': gain(ks[13], (DEPTH, BRANCH_WIDTH)),
        'sgu_ln_b': nrm(ks[14], (DEPTH, BRANCH_WIDTH), 0.01),
        'sgu_w_s': nrm(ks[15], (DEPTH, SGU_GROUPS, SGU_CHUNK, SGU_CHUNK), SGU_CHUNK ** -0.5),
        'sgu_b_s': gain(ks[16], (DEPTH, SGU_GROUPS, SGU_CHUNK)),
        'hg_lb': nrm(ks[17], (DEPTH, BRANCH_WIDTH), 1.0),
        'hg_norm_g': gain(ks[18], (DEPTH, HG_DV)),
        'gla_w_up': nrm(ks[19], (DEPTH, GLA_RANK, GLA_DK_TOTAL), GLA_RANK ** -0.5),
        'gla_b_up': nrm(ks[20], (DEPTH, GLA_DK_TOTAL), 0.1),
        'gla_norm_g': gain(ks[21], (DEPTH, GLA_DV)),
        'w_branch': nrm(ks[22], (DEPTH, N_BRANCH, BRANCH_WIDTH, D_MODEL), BRANCH_WIDTH ** -0.5),
        'w_out': nrm(ks[23], (DEPTH, D_MODEL, D_MODEL), D_MODEL ** -0.5),
        'w_mlp_up': nrm(ks[24], (DEPTH, D_MODEL, D_FF), D_MODEL ** -0.5),
        'w_mlp_down': nrm(ks[25], (DEPTH, D_FF, D_MODEL), D_FF ** -0.5),
    }


def reference(x_prompt, x_sample, state_hgrn, state_gla, c_prompt, c_sample, w_ada, b_ada, g_pre_mix, g_post_mix, g_pre_mlp, g_post_mlp, w_in, sgu_ln_g, sgu_ln_b, sgu_w_s, sgu_b_s, hg_lb, hg_norm_g, gla_w_up, gla_b_up, gla_norm_g, w_branch, w_out, w_mlp_up, w_mlp_down):
    params = {
        'w_ada': w_ada, 'b_ada': b_ada,
        'g_pre_mix': g_pre_mix, 'g_post_mix': g_post_mix,
        'g_pre_mlp': g_pre_mlp, 'g_post_mlp': g_post_mlp,
        'w_in': w_in, 'sgu_ln_g': sgu_ln_g, 'sgu_ln_b': sgu_ln_b,
        'sgu_w_s': sgu_w_s, 'sgu_b_s': sgu_b_s,
        'hg_lb': hg_lb, 'hg_norm_g': hg_norm_g,
        'gla_w_up': gla_w_up, 'gla_b_up': gla_b_up, 'gla_norm_g': gla_norm_g,
        'w_branch': w_branch, 'w_out': w_out,
        'w_mlp_up': w_mlp_up, 'w_mlp_down': w_mlp_down,
    }
    zero_hg = jnp.zeros((DEPTH, BATCH, HG_HEADS, HG_DK, HG_DV), jnp.float32)
    zero_gla = jnp.zeros((DEPTH, BATCH, GLA_HEADS, GLA_DK, GLA_DV), jnp.float32)
    y_prompt, hgrn_prompt, gla_prompt, _ = trunk(x_prompt, c_prompt, zero_hg, zero_gla, params)
    y_sample, hgrn_sample, gla_sample, sgu_v_sample = trunk(x_sample, c_sample, state_hgrn, state_gla, params)
    return (y_prompt, y_sample, hgrn_prompt, gla_prompt, hgrn_sample, gla_sample, sgu_v_sample)
```

```python
import numpy as np
from contextlib import ExitStack
import concourse.bass as bass
import concourse.mybir as mybir
from concourse.bass_utils import run_bass_kernel_spmd

F32 = mybir.dt.float32
BF16 = mybir.dt.bfloat16
AF = mybir.ActivationFunctionType
ALU = mybir.AluOpType
AX = mybir.AxisListType

D = 2048
KC = 16
L_FULL = 4
NPB = 1024
NS = 16
NCOL = NPB + NS
SEQ = 2048
BW = 1024
INW = 15376
O_U, O_V, O_HQ, O_HF, O_HI, O_HG, O_GQ, O_GK, O_GV, O_GR, O_GLR, O_GATE = (
    0, 1024, 2048, 3072, 4096, 5120, 6144, 6656, 7168, 8192, 9216, 9232)
EPS = 1e-6
ENG = ['pe', 'act', 'dve', 'pool', 'sp']
NDSEM = 8


class Prog:
    def __init__(self):
        self.ops = {e: [] for e in ENG}
        self.lastw = {}
        self.rd = {}
        self.floor = set()

    def op(self, eng, fn, r=(), w=(), dma=0, nofloor=False):
        idx = len(self.ops[eng])
        me = (eng, idx)
        deps = set()
        for k in r:
            t = self.lastw.get(k)
            if t is not None:
                deps.add(t)
        for k in w:
            t = self.lastw.get(k)
            if t is not None:
                deps.add(t)
            for t2 in self.rd.get(k, ()):
                deps.add(t2)
        if not nofloor:
            deps |= self.floor
        deps.discard(me)
        for k in r:
            self.rd.setdefault(k, []).append(me)
        for k in w:
            self.lastw[k] = me
            self.rd[k] = []
        self.ops[eng].append({'fn': fn, 'deps': deps, 'dma': dma, 'sig': False})
        return me

    def barrier(self):
        fl = set()
        for e in ENG:
            n = len(self.ops[e])
            if n == 0:
                continue
            if e in ('sp', 'pool'):
                for i in range(max(0, n - NDSEM), n):
                    fl.add((e, i))
            else:
                fl.add((e, n - 1))
        self.floor = fl

    def finalize(self, sems, dsems):
        ops = self.ops
        for e in ENG:
            for rec in ops[e]:
                for (f, i) in rec['deps']:
                    ops[f][i]['sig'] = True
        for e in ENG:
            cnt = 0
            dcum = [0] * NDSEM
            di = 0
            for rec in ops[e]:
                if rec['dma']:
                    j = di % NDSEM
                    rec['prev'] = (dsems[e][j], dcum[j])
                    dcum[j] += 16 * rec['dma']
                    rec['tok'] = (dsems[e][j], dcum[j])
                    di += 1
                else:
                    if rec['sig']:
                        cnt += 1
                    rec['tok'] = (sems[e], cnt)
        for e in ENG:
            known = {}
            for rec in ops[e]:
                waits = {}
                for (f, i) in rec['deps']:
                    prod = ops[f][i]
                    if (not prod['dma']) and f == e and e == 'pe':
                        continue
                    sem, val = prod['tok']
                    key = id(sem)
                    if known.get(key, 0) >= val:
                        continue
                    if key not in waits or waits[key][1] < val:
                        waits[key] = (sem, val)
                if rec['dma']:
                    sem, val = rec['prev']
                    key = id(sem)
                    if val > 0 and known.get(key, 0) < val:
                        if key not in waits or waits[key][1] < val:
                            waits[key] = (sem, val)
                for key, (sem, val) in waits.items():
                    known[key] = val
                rec['waits'] = list(waits.values())

    def emit(self, e, h, sems):
        for rec in self.ops[e]:
            for (sem, val) in rec['waits']:
                h.wait_ge(sem, val)
            ins = rec['fn'](h)
            if rec['dma']:
                assert len(ins) == rec['dma']
                for i_ in ins:
                    i_.then_inc(rec['tok'][0], 16)
            elif rec['sig']:
                assert ins is not None
                ins.then_inc(sems[e], 1)


class StopBuild(Exception):
    pass


def build(depth=L_FULL, nbody=2, stop=None):
    LW = depth
    nc = bass.Bass("TRN2", target_bir_lowering=False)
    P = Prog()
    es = ExitStack()

    def ck(name):
        if stop == name:
            raise StopBuild()

    def din(name, shape):
        return nc.dram_tensor(name, list(shape), F32, kind="ExternalInput").ap()

    def dout(name, shape):
        return nc.dram_tensor(name, list(shape), F32, kind="ExternalOutput").ap()

    def dscr(name, shape):
        return nc.dram_tensor(name, list(shape), F32, kind="Internal").ap()

    xp = din("xp", [SEQ, D])
    xsm = din("xsm", [NS, D])
    cfm = din("cfm", [128, KC, 17])
    sthg = din("sthg", [LW, NS, 8, 128, 128])
    stgl = din("stgl", [LW, NS, 4, 128, 256])
    w_ada = din("w_ada", [LW, D, 6 * D])
    w_in = din("w_in", [LW, D, INW])
    w_br = din("w_br", [LW, 3, BW, D])
    w_out = din("w_out", [LW, D, D])
    w_up = din("w_up", [LW, D, 4 * D])
    w_dn = din("w_dn", [LW, 4 * D, D])
    bada = din("bada", [128, L_FULL, 96])
    gvec = din("gvec", [128, 4, L_FULL, KC])
    lng = din("lng", [L_FULL, 128, BW])
    lnb = din("lnb", [L_FULL, 128, BW])
    wst = din("wst", [L_FULL, 128, 8, 128])
    bsr = din("bsr", [L_FULL, 128, 8, 128])
    w00 = din("w00", [L_FULL, 16, 8])
    hglb = din("hglb", [128, 8, L_FULL])
    hgn = din("hgn", [128, L_FULL])
    gln = din("gln", [128, L_FULL, 2])
    gwup = din("gwup", [L_FULL, 16, 512])
    gbup = din("gbup", [128, L_FULL, 4])
    c_ident = din("c_ident", [128, 128])
    c_tri = din("c_tri", [128, 128])

    yp = dout("yp", [SEQ, D])
    ysm = dout("ysm", [NS, D])
    hgp = dout("hgp", [L_FULL, 8, 128, 128])
    glp = dout("glp", [L_FULL, 4, 128, 256])
    hgs = dout("hgs", [L_FULL, NS, 8, 128, 128])
    gls = dout("gls", [L_FULL, NS, 4, 128, 256])
    sguv = dout("sguv", [L_FULL, NS, BW])

    xs_d = dscr("xs_d", [128, KC * NCOL])
    mod_d = dscr("mod_d", [L_FULL, 128, 96 * 17])
    shg_d = dscr("shg_d", [L_FULL, 8, 128, 128])
    sgl_d = dscr("sgl_d", [L_FULL, 4, 128, 256])

    A_B, B_B, C_B, W_B = KC * NCOL * 4, KC * NCOL * 2, KC * NCOL * 2, 3 * KC * 256 * 2
    T_B = 50000
    arA = es.enter_context(nc.sbuf_tensor("arA", [128, A_B // 4], F32))
    arB = es.enter_context(nc.sbuf_tensor("arB", [128, B_B // 4], F32))
    arC = es.enter_context(nc.sbuf_tensor("arC", [128, C_B // 4], F32))
    arW = es.enter_context(nc.sbuf_tensor("arW", [128, W_B // 4], F32))
    arT = es.enter_context(nc.sbuf_tensor("arT", [128, T_B // 4], F32))
    psb = [es.enter_context(nc.psum_tensor("ps%d" % i, [128, 512], F32)) for i in range(8)]

    def carve(ar, off, parts, shape, dt):
        n = int(np.prod(shape))
        esz = 4 if dt == F32 else 2
        assert off % 4 == 0 and (n * esz) % 4 == 0
        v = ar[0:parts, off // 4: (off + n * esz) // 4]
        if dt != F32:
            v = v.bitcast(dt)
        if len(shape) == 2:
            v = v.rearrange("p (a b) -> p a b", b=shape[1])
        elif len(shape) == 3:
            v = v.rearrange("p (a b c) -> p a b c", b=shape[1], c=shape[2])
        return v

    class Alloc:
        def __init__(self, ar, base, size):
            self.ar, self.base, self.size, self.off = ar, base, size, 0

        def get(self, parts, shape, dt):
            n = int(np.prod(shape)) * (4 if dt == F32 else 2)
            n4 = (n + 3) // 4 * 4
            assert self.off + n4 <= self.size, ("arena overflow", self.off, n4, self.size)
            v = carve(self.ar, self.base + self.off, parts, shape, dt)
            self.off += n4
            return v

        def reset(self):
            self.off = 0

    A = carve(arA, 0, 128, [KC, NCOL], F32)
    Y = carve(arA, 0, 128, [24, NCOL], BF16)
    Bh = carve(arB, 0, 128, [KC, NCOL], BF16)
    Cm = carve(arC, 0, 128, [KC, NCOL], BF16)
    Wb = [carve(arW, s * KC * 256 * 2, 128, [KC, 256], BF16) for s in range(3)]
    alC = Alloc(arC, 0, C_B)
    alAt = Alloc(arA, 24 * NCOL * 2, A_B - 24 * NCOL * 2 - NCOL * 4)
    alT = Alloc(arT, 0, T_B)

    ident = alT.get(128, [128], F32)
    identb = alT.get(128, [128], BF16)
    onesb = alT.get(128, [128], BF16)
    tri = alT.get(128, [128], F32)
    gv_t = alT.get(128, [4, L_FULL, KC], F32)
    bada_t = alT.get(128, [L_FULL, 96], F32)
    lb_t = alT.get(128, [8, L_FULL], F32)
    oml_t = alT.get(128, [8, L_FULL], F32)
    noml_t = alT.get(128, [8, L_FULL], F32)
    hgn_t = alT.get(128, [L_FULL], F32)
    gln_t = alT.get(128, [L_FULL, 2], F32)
    gbup_t = alT.get(128, [L_FULL, 4], F32)
    mod_t = alT.get(128, [96, 17], F32)
    gs1 = alT.get(128, [KC, 17], F32)
    gg1 = alT.get(128, [KC, 17], F32)
    gs2 = alT.get(128, [KC, 17], F32)
    gg2 = alT.get(128, [KC, 17], F32)
    tmp_off = alT.off
    tmp = [alT.get(128, [512], F32) for _ in range(4)]
    St = carve(arT, tmp_off, 128, [2048], F32)
    xt = [alT.get(128, [512], F32) for _ in range(2)]
    sqb = [alT.get(128, [512], BF16) for _ in range(2)]
    rs = alT.get(128, [NCOL], F32)
    S_t = alT.get(128, [256], F32)
    Sb_t = alT.get(128, [256], BF16)
    csc = alT.get(128, [5, 32], F32)
    attsb = alT.get(128, [512], BF16)
    oT = alT.get(128, [2, NCOL], F32)
    acc = oT
    kS = alT.get(16, [128], F32)
    vS = alT.get(16, [256], F32)
    ksc = alT.get(128, [16], F32)
    vSf = alT.get(128, [2, 16], F32)
    smp = alT.get(128, [16 * 8], F32)
    wup_t = alT.get(16, [512], F32)
    glrT = carve(arA, A_B - NCOL * 4, 16, [NCOL], F32)
    T_PERM = alT.off

    cnt = {'ps': 0, 'tmp': 0, 'xt': 0, 'sq': 0, 'w': 0}

    def nps():
        i = cnt['ps'] % 4
        cnt['ps'] += 1
        return i

    def ntmp():
        i = cnt['tmp'] % 4
        cnt['tmp'] += 1
        return i

    def dma(eng, outs_ins, r=(), w=(), nofloor=False):
        def fn(h, oi=outs_ins):
            return [h.dma_start(out=o, in_=i) for (o, i) in oi]
        return P.op(eng, fn, r=r, w=w, dma=len(outs_ins), nofloor=nofloor)

    def act(out, in_, func, r, w, bias=None, scale=None):
        kw = {}
        if bias is not None:
            kw['bias'] = bias
        if scale is not None:
            kw['scale'] = scale
        return P.op('act', lambda h, o=out, i=in_, f=func, kw=kw: h.activation(out=o, in_=i, func=f, **kw), r=r, w=w)

    def tt(out, in0, in1, op, r, w):
        return P.op('dve', lambda h, o=out, a=in0, b=in1, op=op: h.tensor_tensor(out=o, in0=a, in1=b, op=op), r=r, w=w)

    def ts(out, in0, s1, s2, op0, op1, r, w):
        return P.op('dve', lambda h, o=out, a=in0, s1=s1, s2=s2, op0=op0, op1=op1:
                    h.tensor_scalar(out=o, in0=a, scalar1=s1, scalar2=s2, op0=op0, op1=op1), r=r, w=w)

    def stt(out, in0, sc, in1, op0, op1, r, w):
        return P.op('dve', lambda h, o=out, a=in0, s=sc, b=in1, op0=op0, op1=op1:
                    h.scalar_tensor_tensor(out=o, in0=a, scalar=s, in1=b, op0=op0, op1=op1), r=r, w=w)

    def cpy(out, in_, r, w, eng='dve'):
        if eng == 'dve':
            return P.op('dve', lambda h, o=out, i=in_: h.tensor_copy(out=o, in_=i), r=r, w=w)
        return act(out, in_, AF.Copy, r, w)

    def recip(out, in_, r, w):
        return P.op('dve', lambda h, o=out, i=in_: h.reciprocal(out=o, in_=i), r=r, w=w)

    def mm(out, lhsT, rhs, start, stop, r, w):
        return P.op('pe', lambda h, o=out, a=lhsT, b=rhs, st=start, sp=stop:
                    h.matmul(o, lhsT=a, rhs=b, start=st, stop=sp), r=r, w=w)

    def tr(out, in_, idn, r, w):
        return P.op('pe', lambda h, o=out, i=in_, d=idn: h.transpose(out=o, in_=i, identity=d), r=r, w=w)

    def rsqrt_inplace(t_ap, src_ap, scale, r, w):
        act(t_ap, src_ap, AF.Sqrt, r=r, w=w, bias=EPS, scale=scale)
        recip(t_ap, t_ap, r=w, w=w)

    def wload(src2d, nkc, ncols):
        i = cnt['w']
        cnt['w'] += 1
        s = i % 3
        dst = Wb[s][:, 0:nkc, 0:ncols]
        srcv = src2d.rearrange("(kc p) c -> p kc c", p=128)
        P.op('pool', lambda h, o=dst, i_=srcv: [h.dma_start(out=o, in_=i_)], w=[('w', s)], dma=1, nofloor=True)
        return Wb[s], ('w', s)

    def gen():
        dma('sp', [(ident[:, :], c_ident[:, :]), (tri[:, :], c_tri[:, :]),
                   (gv_t[:, :, :, :], gvec[:, :, :, :]), (bada_t[:, :, :], bada[:, :, :]),
                   (lb_t[:, :, :], hglb[:, :, :]), (hgn_t[:, :], hgn[:, :]), (gln_t[:, :, :], gln[:, :, :]),
                   (gbup_t[:, :, :], gbup[:, :, :])], w=[('const',)])
        cpy(identb[:, :], ident[:, :], r=[('const',)], w=[('identb',)])
        P.op('dve', lambda h: h.memset(onesb[:, :], 1.0), w=[('onesb',)])
        lbx = csc[:, 0, 0:8]
        P.op('dve', lambda h: h.tensor_reduce(out=lbx, in_=lb_t[:, :, :], axis=AX.X, op=ALU.max), r=[('const',)], w=[('lbx',)])
        tt(lb_t[:, :, :], lb_t[:, :, :], lbx.unsqueeze(2).to_broadcast([128, 8, L_FULL]), ALU.subtract, r=[('const',), ('lbx',)], w=[('lb',)])
        act(lb_t[:, :, :], lb_t[:, :, :], AF.Exp, r=[('lb',)], w=[('lb',)])
        P.op('dve', lambda h: h.tensor_reduce(out=lbx, in_=lb_t[:, :, :], axis=AX.X, op=ALU.add), r=[('lb',)], w=[('lbx',)])
        recip(lbx, lbx, r=[('lbx',)], w=[('lbx',)])
        tt(lb_t[:, :, :], lb_t[:, :, :], lbx.unsqueeze(2).to_broadcast([128, 8, L_FULL]), ALU.mult, r=[('lb',), ('lbx',)], w=[('lb',)])
        P.op('dve', lambda h: h.memset(lb_t[:, :, 0:1], 0.0), r=[('lb',)], w=[('lb',)])
        for l_ in range(2, L_FULL):
            tt(lb_t[:, :, l_:l_ + 1], lb_t[:, :, l_:l_ + 1], lb_t[:, :, l_ - 1:l_], ALU.add, r=[('lb',)], w=[('lb',)])
        ts(oml_t[:, :, :], lb_t[:, :, :], -1.0, 1.0, ALU.mult, ALU.add, r=[('lb',)], w=[('oml',)])
        ts(noml_t[:, :, :], oml_t[:, :, :], -1.0, None, ALU.mult, ALU.bypass, r=[('oml',)], w=[('noml',)])

        cs_f = carve(arC, 0, 128, [KC, 17], F32)
        cs_b = carve(arC, KC * 17 * 4, 128, [KC, 17], BF16)
        dma('sp', [(cs_f[:, :, :], cfm[:, :, :])], w=[('csf',)])
        act(cs_b[:, :, :], cs_f[:, :, :], AF.Silu, r=[('csf',)], w=[('csb',)])
        for l in range(depth):
            for u in range(48):
                wt, wk = wload(w_ada[l, :, u * 256:(u + 1) * 256], KC, 256)
                for mc in range(2):
                    b = nps()
                    for kc in range(KC):
                        mm(psb[b][:, 0:17], wt[:, kc, mc * 128:(mc + 1) * 128], cs_b[:, kc, :], kc == 0, kc == KC - 1,
                           r=[wk, ('csb',)], w=[('ps', b)])
                    ch = u * 2 + mc
                    act(mod_t[:, ch, :], psb[b][:, 0:17], AF.Identity, r=[('ps', b), ('const',)], w=[('mod',)],
                        bias=bada_t[:, l, ch:ch + 1])
            dma('sp', [(mod_d[l].rearrange("p (a b) -> p a b", b=17), mod_t[:, :, :])], r=[('mod',)], w=[('modd', l)])
        P.barrier()
        ck('pro')

        def col_tiles(body):
            cts = [(0, 512), (512, 512)]
            if body == 0:
                cts.append((NPB, NS))
            return cts

        def Akey(kc, ci):
            return ('A', kc, ci)

        def stats_rs(src, cts, keyf, nfeat_chunks, scale):
            for ci, (c0, n) in enumerate(cts):
                b = nps()
                for kc in range(nfeat_chunks):
                    q = cnt['sq'] % 2
                    cnt['sq'] += 1
                    act(sqb[q][:, 0:n], src[:, kc, c0:c0 + n], AF.Square, r=[keyf(kc, ci)], w=[('sq', q)])
                    mm(psb[b][:, 0:n], onesb[:, :], sqb[q][:, 0:n], kc == 0, kc == nfeat_chunks - 1,
                       r=[('sq', q), ('onesb',)], w=[('ps', b)])
                act(rs[:, c0:c0 + n], psb[b][:, 0:n], AF.Sqrt, r=[('ps', b)], w=[('rs', ci)], bias=EPS, scale=scale)
                recip(rs[:, c0:c0 + n], rs[:, c0:c0 + n], r=[('rs', ci)], w=[('rs', ci)])

        def norm_to_h(cts, GS, SH, shoff):
            stats_rs(A, cts, Akey, KC, 1.0 / D)
            for ci, (c0, n) in enumerate(cts):
                for kc in range(KC):
                    t = ntmp()
                    tt(tmp[t][:, 0:n], A[:, kc, c0:c0 + n], rs[:, c0:c0 + n], ALU.mult,
                       r=[Akey(kc, ci), ('rs', ci)], w=[('tmp', t)])
                    if n == 512:
                        act(Bh[:, kc, c0:c0 + n], tmp[t][:, 0:n], AF.Identity, r=[('tmp', t), ('gsh',)], w=[('B', kc, ci)],
                            bias=mod_t[:, shoff + kc, 0:1], scale=GS[:, kc, 0:1])
                    else:
                        tt(tmp[t][:, 0:n], tmp[t][:, 0:n], GS[:, kc, 1:17], ALU.mult, r=[('tmp', t), ('gsh',)], w=[('tmp', t)])
                        tt(Bh[:, kc, c0:c0 + n], tmp[t][:, 0:n], mod_t[:, shoff + kc, 1:17], ALU.add,
                           r=[('tmp', t), ('gsh',)], w=[('B', kc, ci)])

        def store_x(cts):
            ncb_ = cts[-1][0] + cts[-1][1]
            dma('sp', [(xs_d[:, kc * NCOL: kc * NCOL + ncb_], A[:, kc, 0:ncb_]) for kc in range(KC)],
                r=[Akey(kc, ci) for kc in range(KC) for ci in range(len(cts))], w=[('xs',)])

        def resid_update(cts, GG):
            stats_rs(A, cts, Akey, KC, 1.0 / D)
            for ci, (c0, n) in enumerate(cts):
                for kc in range(KC):
                    q = cnt['xt'] % 2
                    cnt['xt'] += 1
                    dma('sp', [(xt[q][:, 0:n], xs_d[:, kc * NCOL + c0: kc * NCOL + c0 + n])], r=[('xs',)], w=[('xt', q)])
                    t = ntmp()
                    tt(tmp[t][:, 0:n], A[:, kc, c0:c0 + n], rs[:, c0:c0 + n], ALU.mult,
                       r=[Akey(kc, ci), ('rs', ci)], w=[('tmp', t)])
                    if n == 512:
                        stt(A[:, kc, c0:c0 + n], tmp[t][:, 0:n], GG[:, kc, 0:1], xt[q][:, 0:n], ALU.mult, ALU.add,
                            r=[('tmp', t), ('xt', q), ('gsh',)], w=[Akey(kc, ci)])
                    else:
                        tt(tmp[t][:, 0:n], tmp[t][:, 0:n], GG[:, kc, 1:17], ALU.mult, r=[('tmp', t), ('gsh',)], w=[('tmp', t)])
                        tt(A[:, kc, c0:c0 + n], tmp[t][:, 0:n], xt[q][:, 0:n], ALU.add,
                           r=[('tmp', t), ('xt', q)], w=[Akey(kc, ci)])

        def proj(src, srckey, nkc, wsrc2d, ncols, cts, evac):
            wt, wk = wload(wsrc2d, nkc, ncols)
            nmc = (ncols + 127) // 128
            for mc in range(nmc):
                mw = min(128, ncols - mc * 128)
                for ci, (c0, n) in enumerate(cts):
                    b = nps()
                    for kc in range(nkc):
                        mm(psb[b][0:mw, 0:n], wt[:, kc, mc * 128:mc * 128 + mw], src[:, kc, c0:c0 + n], kc == 0, kc == nkc - 1,
                           r=[wk, srckey(kc, ci)], w=[('ps', b)])
                    evac(mc, ci, c0, n, b)

        def Bkey(kc, ci):
            return ('B', kc, ci)

        def gelu_evac(dst, ps_ap, parts, n, b, wkeys):
            t1, t2 = ntmp(), ntmp()
            a, c = tmp[t1][0:parts, 0:n], tmp[t2][0:parts, 0:n]
            act(a, ps_ap, AF.Square, r=[('ps', b)], w=[('tmp', t1)])
            ts(a, a, 0.044715, 1.0, ALU.mult, ALU.add, r=[('tmp', t1)], w=[('tmp', t1)])
            tt(a, a, ps_ap, ALU.mult, r=[('tmp', t1), ('ps', b)], w=[('tmp', t1)])
            act(c, a, AF.Sigmoid, r=[('tmp', t1)], w=[('tmp', t2)], scale=1.5957691216057308)
            tt(dst, c, ps_ap, ALU.mult, r=[('tmp', t2), ('ps', b)], w=wkeys)

        def recur(body, l, cts, qf, kf, la, vT, sg, nvc, C, sc_l, kscale, gn_ap, ybase, st_in, st_scr, st_out, smp_in, smp_out, al, alt):
            dv = nvc * 128
            NCH = NPB // C
            half = C // 2
            off_rel = al.off
            rel = al.get(128, [NPB], F32)
            Ee = al.get(128, [NPB], F32)
            vdg = carve(al.ar, al.base + off_rel, 16, [2048], F32)
            Qt = al.get(128, [NPB], BF16)
            Kt = al.get(128, [NPB], BF16)
            Ktm = al.get(C, [NCH, 128], BF16)
            Vtm = alt.get(C, [NCH, dv], BF16)
            Bs, rho, sig_, gam, scr = (csc[:, i, 0:NCH] for i in range(5))
            P.op('dve', lambda h: h.memset(rel[:, :], 1.0), w=[('rel',)])
            Bc = Ee
            P.op('dve', lambda h, o=Bc[:, :], d0=rel[:, :], d1=la[:, 0:NPB]: h.tensor_tensor_scan(out=o, data0=d0, data1=d1, initial=0.0, op0=ALU.mult, op1=ALU.add),
                 r=[('la',), ('rel',)], w=[('Ee',)])
            Bc3 = Bc[:, :].rearrange("p (c j) -> p c j", j=C)
            bref = Bc3[:, :, half - 1:half]
            bend = Bc3[:, :, C - 1:C]
            tt(rel[:, :].rearrange("p (c j) -> p c j", j=C), Bc3, bref.to_broadcast([128, NCH, C]), ALU.subtract, r=[('Ee',)], w=[('rel',)])
            P.op('dve', lambda h, o=Bs: h.memset(o[:, 0:1], 0.0), w=[('Bs',)])
            if NCH > 1:
                cpy(Bs[:, 1:NCH].unsqueeze(2), bend[:, 0:NCH - 1, :], r=[('Ee',)], w=[('Bs',)])
            tt(scr.unsqueeze(2), bref, Bs.unsqueeze(2), ALU.subtract, r=[('Ee',), ('Bs',)], w=[('scr',)])
            act(rho, scr, AF.Exp, r=[('scr',)], w=[('rho',)], scale=sc_l)
            tt(scr.unsqueeze(2), bend, bref, ALU.subtract, r=[('Ee',), ('rho',)], w=[('scr',)])
            act(sig_, scr, AF.Exp, r=[('scr',)], w=[('sig',)], scale=sc_l)
            tt(scr.unsqueeze(2), bend, Bs.unsqueeze(2), ALU.subtract, r=[('Ee',), ('Bs',), ('sig',)], w=[('scr',)])
            act(gam, scr, AF.Exp, r=[('scr',)], w=[('gam',)], scale=sc_l)
            act(Ee[:, :], rel[:, :], AF.Exp, r=[('rel',), ('gam',)], w=[('Ee',)], scale=sc_l)
            tt(Qt[:, :], qf[:, 0:NPB], Ee[:, :], ALU.mult, r=[('qf',), ('Ee',)], w=[('Qt',)])
            act(Ee[:, :], rel[:, :], AF.Exp, r=[('rel',), ('Qt',)], w=[('Ee',)], scale=-sc_l)
            stt(Kt[:, :], kf[:, 0:NPB], kscale, Ee[:, :], ALU.mult, ALU.mult, r=[('kf',), ('Ee',)], w=[('Kt',)])

            ck('r1')
            def tm_transposes(srcf, dst, width, skey, dkey):
                items = [(c, vc) for c in range(NCH) for vc in range(width // 128)]
                dflat = dst.rearrange("p c v -> p (c v)")
                for i0 in range(0, len(items), 8):
                    b = nps()
                    pv = psb[b][:, :].bitcast(BF16)
                    grp = items[i0:i0 + 8]
                    for j, (c, vc) in enumerate(grp):
                        tr(pv[0:C, j * 128:(j + 1) * 128], srcf(c, vc), identb[:, :], r=[('identb',), skey], w=[('ps', b)])
                    cpy(dflat[:, i0 * 128:(i0 + len(grp)) * 128], pv[0:C, 0:len(grp) * 128], r=[('ps', b)], w=[dkey],
                        eng='act' if (i0 // 8) % 2 else 'dve')
            tm_transposes(lambda c, vc: Kt[:, c * C:(c + 1) * C], Ktm, 128, ('Kt',), ('Ktm',))
            tm_transposes(lambda c, vc: vT[:, vc, c * C:(c + 1) * C], Vtm, dv, ('vT',), ('Vtm',))
            if st_in is None:
                P.op('dve', lambda h: h.memset(S_t[:, 0:dv], 0.0), w=[('S',)])
            else:
                dma('sp', [(S_t[:, 0:dv], st_in)], r=[('sscr', l)], w=[('S',)])
            act(Sb_t[:, 0:dv], S_t[:, 0:dv], AF.Identity, r=[('S',), ('rho',)], w=[('Sb',)], scale=rho[:, 0:1])
            cpb = 512 // C
            for c0_ in range(0, NCH, cpb):
                nchb = min(cpb, NCH - c0_)
                b = nps()
                for j in range(nchb):
                    c = c0_ + j
                    mm(psb[b][0:C, j * C:(j + 1) * C], Kt[:, c * C:(c + 1) * C], Qt[:, c * C:(c + 1) * C], True, True,
                       r=[('Kt',), ('Qt',)], w=[('ps', b)])
                tt(attsb[0:C, 0:nchb * C].rearrange("p (a b) -> p a b", b=C), psb[b][0:C, 0:nchb * C].rearrange("p (a b) -> p a b", b=C),
                   tri[0:C, 0:C].unsqueeze(1).to_broadcast([C, nchb, C]), ALU.mult, r=[('ps', b), ('const',)], w=[('att',)])
                bo = [5, 6][0:nvc]
                for j in range(nchb):
                    c = c0_ + j
                    col = j * C
                    for vc in range(nvc):
                        mm(psb[bo[vc]][:, col:col + C], Vtm[:, c, vc * 128:(vc + 1) * 128], attsb[0:C, j * C:(j + 1) * C], True, False,
                           r=[('Vtm',), ('att',)], w=[('ps', bo[vc])])
                        mm(psb[bo[vc]][:, col:col + C], Sb_t[:, vc * 128:(vc + 1) * 128], Qt[:, c * C:(c + 1) * C], False, True,
                           r=[('Sb',), ('Qt',)], w=[('ps', bo[vc])])
                    bu = nps()
                    mm(psb[bu][:, 0:dv], Ktm[:, c, :], Vtm[:, c, :], True, True, r=[('Ktm',), ('Vtm',)], w=[('ps', bu)])
                    act(S_t[:, 0:dv], S_t[:, 0:dv], AF.Identity, r=[('S',), ('gam',), ('Sb',)], w=[('S',)], scale=gam[:, c:c + 1])
                    stt(S_t[:, 0:dv], psb[bu][:, 0:dv], sig_[:, c:c + 1], S_t[:, 0:dv], ALU.mult, ALU.add, r=[('ps', bu), ('S',), ('sig',)], w=[('S',)])
                    if c + 1 < NCH:
                        act(Sb_t[:, 0:dv], S_t[:, 0:dv], AF.Identity, r=[('S',), ('rho',)], w=[('Sb',)], scale=rho[:, c + 1:c + 2])
                for vc in range(nvc):
                    cpy(oT[:, vc, c0_ * C:(c0_ + nchb) * C], psb[bo[vc]][:, 0:nchb * C], r=[('ps', bo[vc])], w=[('oT',)], eng='act')
            if st_scr is not None:
                dma('sp', [(st_scr, S_t[:, 0:dv])], r=[('S',)], w=[('sscr', l)])
            if st_out is not None:
                dma('sp', [(st_out, S_t[:, 0:dv])], r=[('S',)], w=[('stout',)])
            if smp_in is not None:
                P.barrier()
                nsb = 2048 // dv
                fS = smp[:, 0:16]
                act(fS, la[:, NPB:NCOL], AF.Exp, r=[('la',)], w=[('fS',)], scale=sc_l)
                ts(ksc[:, :], kf[:, NPB:NCOL], kscale, None, ALU.mult, ALU.bypass, r=[('kf',)], w=[('ksc',)])
                b = nps()
                tr(psb[b][0:16, 0:128], ksc[:, :], ident[:, :], r=[('ksc',), ('const',)], w=[('ps', b)])
                cpy(kS[:, :], psb[b][0:16, 0:128], r=[('ps', b)], w=[('kS',)])
                cpy(vSf[:, 0:nvc, :], vT[:, :, NPB:NCOL], r=[('vT',)], w=[('vSf',)])
                b = nps()
                for vc in range(nvc):
                    tr(psb[b][0:16, vc * 128:(vc + 1) * 128], vSf[:, vc, :], ident[:, :], r=[('vSf',), ('const',)], w=[('ps', b)])
                cpy(vS[:, 0:dv], psb[b][0:16, 0:dv], r=[('ps', b)], w=[('vS',)])
                for sb_ in range(NS // nsb):
                    j0 = sb_ * nsb
                    St3 = St[:, 0:nsb * dv].rearrange("p (j v) -> p j v", v=dv)
                    dma('sp', [(St3, smp_in[j0:j0 + nsb].rearrange("j d v -> d j v"))], w=[('St',)])
                    tt(St3, St3, fS[:, j0:j0 + nsb].unsqueeze(2).to_broadcast([128, nsb, dv]), ALU.mult, r=[('St',), ('fS',)], w=[('St',)])
                    tt(vdg[:, 0:nsb * dv].rearrange("p (j v) -> p j v", v=dv), vS[:, 0:dv].unsqueeze(1).to_broadcast([16, nsb, dv]),
                       ident[0:16, j0:j0 + nsb].unsqueeze(2).to_broadcast([16, nsb, dv]), ALU.mult, r=[('vS',), ('const',), ('Ee',), ('rel',)], w=[('vdg',)])
                    for q0 in range(0, nsb * dv, 512):
                        b = nps()
                        mm(psb[b][:, :], kS[:, :], vdg[:, q0:q0 + 512], True, True, r=[('kS',), ('vdg',)], w=[('ps', b)])
                        tt(St[:, q0:q0 + 512], St[:, q0:q0 + 512], psb[b][:, :], ALU.add, r=[('St',), ('ps', b)], w=[('St',)])
                    dma('sp', [(smp_out[j0:j0 + nsb].rearrange("j d v -> d j v"), St3)], r=[('St',)], w=[('smpout',)])
                    for vc in range(nvc):
                        b = nps()
                        for j in range(nsb):
                            mm(psb[b][:, j:j + 1], St3[:, j, vc * 128:(vc + 1) * 128], qf[:, NPB + j0 + j:NPB + j0 + j + 1], True, True,
                               r=[('St',), ('qf',)], w=[('ps', b)])
                        cpy(oT[:, vc, NPB + j0:NPB + j0 + nsb], psb[b][:, 0:nsb], r=[('ps', b)], w=[('oT',)])
            if smp_in is not None:
                P.barrier()
            stats_rs(oT, cts, lambda kc, ci: ('oT',), nvc, 1.0 / dv)
            for ci, (c0, n) in enumerate(cts):
                for vc in range(nvc):
                    t = ntmp()
                    tt(tmp[t][:, 0:n], oT[:, vc, c0:c0 + n], rs[:, c0:c0 + n], ALU.mult, r=[('oT',), ('rs', ci)], w=[('tmp', t)])
                    stt(Y[:, ybase + vc, c0:c0 + n], tmp[t][:, 0:n], gn_ap(vc), sg[:, vc, c0:c0 + n], ALU.mult, ALU.mult,
                        r=[('tmp', t), ('sg',), ('const',)], w=[('Y', ybase + vc, ci)])

        for body in range(nbody):
            cts = col_tiles(body)
            nct = len(cts)
            ncb = cts[-1][0] + cts[-1][1]
            P.barrier()
            xin = [carve(arC, i * D * 4, 128, [D], F32) for i in range(2)]
            for blk in range(NPB // 128):
                q = blk % 2
                dma('sp', [(xin[q][:, :], xp[body * NPB + blk * 128: body * NPB + (blk + 1) * 128, :])], w=[('xin', q)])
                for k4 in range(0, KC, 4):
                    b = nps()
                    for j in range(4):
                        kc = k4 + j
                        tr(psb[b][:, j * 128:(j + 1) * 128], xin[q][:, kc * 128:(kc + 1) * 128], ident[:, :], r=[('xin', q), ('const',)], w=[('ps', b)])
                    cpy(A[:, k4:k4 + 4, blk * 128:(blk + 1) * 128], psb[b][:, :].rearrange("p (a b) -> p a b", b=128),
                        r=[('ps', b)], w=[Akey(kc_, blk // 4) for kc_ in range(k4, k4 + 4)], eng='act' if (k4 // 4) % 2 else 'dve')
            if body == 0:
                dma('sp', [(xin[0][0:NS, :], xsm[:, :])], w=[('xin', 0)])
                b = nps()
                for kc in range(KC):
                    tr(psb[b][:, kc * 16:(kc + 1) * 16], xin[0][0:NS, kc * 128:(kc + 1) * 128], ident[0:NS, 0:NS], r=[('xin', 0), ('const',)], w=[('ps', b)])
                cpy(A[:, :, NPB:NCOL], psb[b][:, 0:KC * 16].rearrange("p (a b) -> p a b", b=16), r=[('ps', b)], w=[Akey(kc_, 2) for kc_ in range(KC)])
            P.barrier()

            for l in range(depth):
                dma('sp', [(mod_t[:, :, :], mod_d[l].rearrange("p (a b) -> p a b", b=17))], r=[('modd', l)], w=[('mod',)])
                gpre1 = gv_t[:, 0, l, :].unsqueeze(2).to_broadcast([128, KC, 17])
                gpost1 = gv_t[:, 1, l, :].unsqueeze(2).to_broadcast([128, KC, 17])
                gpre2 = gv_t[:, 2, l, :].unsqueeze(2).to_broadcast([128, KC, 17])
                gpost2 = gv_t[:, 3, l, :].unsqueeze(2).to_broadcast([128, KC, 17])
                stt(gs1[:, :, :], mod_t[:, 16:32, :], 1.0, gpre1, ALU.add, ALU.mult, r=[('mod',), ('const',)], w=[('gsh',)])
                tt(gg1[:, :, :], mod_t[:, 32:48, :], gpost1, ALU.mult, r=[('mod',), ('const',)], w=[('gsh',)])
                stt(gs2[:, :, :], mod_t[:, 64:80, :], 1.0, gpre2, ALU.add, ALU.mult, r=[('mod',), ('const',)], w=[('gsh',)])
                tt(gg2[:, :, :], mod_t[:, 80:96, :], gpost2, ALU.mult, r=[('mod',), ('const',)], w=[('gsh',)])
                ck('xload')
                store_x(cts)
                norm_to_h(cts, gs1, None, 0)
                P.barrier()
                ck('norm1')
                alC.reset(); alAt.reset()
                vt = alC.get(128, [8, BW], BF16)
                lnt = alC.get(128, [BW], F32)
                vts = alC.get(16, [BW], F32)
                vtsb = alC.get(16, [BW], BF16)
                bst = alC.get(128, [16], F32)
                mv = alC.get(128, [4], F32)
                wstf = alC.get(128, [8, 128], F32)
                wstb = alC.get(128, [8, 128], BF16)
                dgb = alC.get(16, [8, 16], BF16)
                LG = alAt.get(128, [BW], F32)
                LB = alAt.get(128, [BW], F32)
                BS = alAt.get(128, [8, 128], F32)
                W00 = alAt.get(16, [8], F32)
                dma('sp', [(LG[:, :], lng[l]), (LB[:, :], lnb[l]), (BS[:, :, :], bsr[l]), (wstf[:, :, :], wst[l]), (W00[:, :], w00[l])], w=[('sguc',)])
                tt(wstb[:, :, :], wstf[:, :, :], tri[:, :].unsqueeze(1).to_broadcast([128, 8, 128]), ALU.mult, r=[('sguc',), ('const',)], w=[('wstb',)])
                tt(dgb[:, :, :], ident[0:16, 0:16].unsqueeze(1).to_broadcast([16, 8, 16]), W00[:, :].unsqueeze(2).to_broadcast([16, 8, 16]), ALU.mult,
                   r=[('sguc',), ('const',)], w=[('dgb',)])
                for u in range(4):
                    def ev(mc, ci, c0, n, b, u=u):
                        gelu_evac(Y[:, u * 2 + mc, c0:c0 + n], psb[b][:, 0:n], 128, n, b, [('Y', u * 2 + mc, ci)])
                    proj(Bh, Bkey, KC, w_in[l, :, O_U + u * 256: O_U + (u + 1) * 256], 256, cts, ev)
                blocks = [(i * 128, 128) for i in range(NPB // 128)] + ([(NPB, NS)] if body == 0 else [])
                for u in range(4):
                    wt, wk = wload(w_in[l, :, O_V + u * 256: O_V + (u + 1) * 256], KC, 256)
                    for bi, (t0, m) in enumerate(blocks):
                        b = nps()
                        for kc in range(KC):
                            mm(psb[b][0:m, 0:256], Bh[:, kc, t0:t0 + m], wt[:, kc, 0:256], kc == 0, kc == KC - 1,
                               r=[wk, Bkey(kc, min(t0 // 512, 2))], w=[('ps', b)])
                        dst = vt[:, bi, u * 256:(u + 1) * 256] if m == 128 else vts[:, u * 256:(u + 1) * 256]
                        gelu_evac(dst, psb[b][0:m, 0:256], m, 256, b, [('vt', bi)])
                for bi, (t0, m) in enumerate(blocks):
                    src = vt[:, bi, :] if m == 128 else vts[:, :]
                    st6 = bst[0:m, 0:12].rearrange("p (a b) -> p a b", b=6)
                    for hh in range(2):
                        P.op('dve', lambda h, o=st6[:, hh, :], i=src[:, hh * 512:(hh + 1) * 512]: h.bn_stats(out=o, in_=i), r=[('vt', bi)], w=[('bst',)])
                    P.op('dve', lambda h, o=mv[0:m, 0:2], i=bst[0:m, 0:12]: h.bn_aggr(out=o, in_=i), r=[('bst',)], w=[('mv',)])
                    act(mv[0:m, 2:3], mv[0:m, 1:2], AF.Sqrt, r=[('mv',)], w=[('mv2',)], bias=EPS)
                    recip(mv[0:m, 2:3], mv[0:m, 2:3], r=[('mv2',)], w=[('mv2',)])
                    ts(lnt[0:m, :], src, mv[0:m, 0:1], mv[0:m, 2:3], ALU.subtract, ALU.mult, r=[('vt', bi), ('mv',), ('mv2',)], w=[('lnt',)])
                    tt(lnt[0:m, :], lnt[0:m, :], LG[0:m, :], ALU.mult, r=[('lnt',), ('sguc',)], w=[('lnt',)])
                    if m == 128:
                        tt(vt[:, bi, :], lnt[:, :], LB[:, :], ALU.add, r=[('lnt',), ('sguc',)], w=[('vt', bi)])
                    else:
                        tt(vts[:, :], lnt[0:m, :], LB[0:m, :], ALU.add, r=[('lnt',), ('sguc',)], w=[('vt', bi)])
                        cpy(vtsb[:, :], vts[:, :], r=[('vt', bi)], w=[('vtsb',)])
                        dma('sp', [(sguv[l], vts[:, :])], r=[('vt', bi)], w=[('sguvout',)])
                for g in range(8):
                    for ci, (c0, n) in enumerate(cts):
                        b = nps()
                        t = ntmp()
                        if n == 512:
                            for j in range(4):
                                bi = c0 // 128 + j
                                mm(psb[b][:, j * 128:(j + 1) * 128], vt[:, bi, g * 128:(g + 1) * 128], wstb[:, g, :], True, True,
                                   r=[('vt', bi), ('wstb',)], w=[('ps', b)])
                            tt(tmp[t][:, :].rearrange("p (a b) -> p a b", b=128), psb[b][:, :].rearrange("p (a b) -> p a b", b=128),
                               BS[:, g, :].unsqueeze(1).to_broadcast([128, 4, 128]), ALU.add, r=[('ps', b), ('sguc',)], w=[('tmp', t)])
                        else:
                            mm(psb[b][:, 0:NS], vtsb[:, g * 128:(g + 1) * 128], dgb[:, g, :], True, True, r=[('vtsb',), ('dgb',)], w=[('ps', b)])
                            tt(tmp[t][:, 0:NS], psb[b][:, 0:NS], BS[:, g, 0:1].to_broadcast([128, NS]), ALU.add, r=[('ps', b), ('sguc',)], w=[('tmp', t)])
                        tt(Y[:, g, c0:c0 + n], tmp[t][:, 0:n], Y[:, g, c0:c0 + n], ALU.mult, r=[('tmp', t), ('Y', g, ci)], w=[('Y', g, ci)])
                P.barrier()
                ck('sgu')
                for hd in range(8):
                    alC.reset(); alAt.reset()
                    la = alC.get(128, [NCOL], F32)
                    qf = alC.get(128, [NCOL], F32)
                    kf = alC.get(128, [NCOL], F32)
                    vT = alAt.get(128, [1, NCOL], BF16)
                    sg = alAt.get(128, [1, NCOL], BF16)
                    def ev_q(mc, ci, c0, n, b):
                        act(qf[:, c0:c0 + n], psb[b][:, 0:n], AF.Silu, r=[('ps', b)], w=[('qf',)])
                    proj(Bh, Bkey, KC, w_in[l, :, O_HQ + hd * 128: O_HQ + (hd + 1) * 128], 128, cts, ev_q)
                    def ev_f(mc, ci, c0, n, b):
                        t = ntmp()
                        act(tmp[t][:, 0:n], psb[b][:, 0:n], AF.Sigmoid, r=[('ps', b)], w=[('tmp', t)])
                        act(la[:, c0:c0 + n], tmp[t][:, 0:n], AF.Ln, r=[('tmp', t), ('oml',), ('lb',)], w=[('la',)],
                            bias=lb_t[:, hd, l:l + 1], scale=oml_t[:, hd, l:l + 1])
                        ts(kf[:, c0:c0 + n], tmp[t][:, 0:n], noml_t[:, hd, l:l + 1], oml_t[:, hd, l:l + 1], ALU.mult, ALU.add,
                           r=[('tmp', t), ('oml',), ('noml',)], w=[('kf',)])
                    proj(Bh, Bkey, KC, w_in[l, :, O_HF + hd * 128: O_HF + (hd + 1) * 128], 128, cts, ev_f)
                    def ev_v(mc, ci, c0, n, b):
                        cpy(vT[:, 0, c0:c0 + n], psb[b][:, 0:n], r=[('ps', b)], w=[('vT',)], eng='act')
                    proj(Bh, Bkey, KC, w_in[l, :, O_HI + hd * 128: O_HI + (hd + 1) * 128], 128, cts, ev_v)
                    def ev_g(mc, ci, c0, n, b):
                        act(sg[:, 0, c0:c0 + n], psb[b][:, 0:n], AF.Silu, r=[('ps', b)], w=[('sg',)])
                    proj(Bh, Bkey, KC, w_in[l, :, O_HG + hd * 128: O_HG + (hd + 1) * 128], 128, cts, ev_g)
                    recur(body, l, cts, qf, kf, la, vT, sg, 1, 32, 1.0, 1.0, lambda vc: hgn_t[:, l:l + 1], 8 + hd,
                          None if body == 0 else shg_d[l, hd],
                          shg_d[l, hd] if body < nbody - 1 else None,
                          hgp[l, hd] if body == nbody - 1 else None,
                          sthg[l, :, hd] if body == 0 else None,
                          hgs[l, :, hd] if body == 0 else None, alC, alAt)
                    P.barrier()
                ck('hgrn')
                dma('sp', [(wup_t[:, :], gwup[l])], w=[('wup',)])
                def ev_glr(mc, ci, c0, n, b):
                    cpy(glrT[:, c0:c0 + n], psb[b][0:16, 0:n], r=[('ps', b)], w=[('glr',)])
                proj(Bh, Bkey, KC, w_in[l, :, O_GLR: O_GLR + 16], 16, cts, ev_glr)
                for hd in range(4):
                    alC.reset(); alAt.reset()
                    la = alC.get(128, [NCOL], F32)
                    qf = alC.get(128, [NCOL], F32)
                    kf = alC.get(128, [NCOL], F32)
                    vT = alAt.get(128, [2, NCOL], BF16)
                    sg = alAt.get(128, [2, NCOL], BF16)
                    def ev_q(mc, ci, c0, n, b):
                        cpy(qf[:, c0:c0 + n], psb[b][:, 0:n], r=[('ps', b)], w=[('qf',)], eng='act')
                    proj(Bh, Bkey, KC, w_in[l, :, O_GQ + hd * 128: O_GQ + (hd + 1) * 128], 128, cts, ev_q)
                    def ev_k(mc, ci, c0, n, b):
                        cpy(kf[:, c0:c0 + n], psb[b][:, 0:n], r=[('ps', b)], w=[('kf',)], eng='act')
                    proj(Bh, Bkey, KC, w_in[l, :, O_GK + hd * 128: O_GK + (hd + 1) * 128], 128, cts, ev_k)
                    for ci, (c0, n) in enumerate(cts):
                        b = nps()
                        t = ntmp()
                        mm(psb[b][:, 0:n], wup_t[:, hd * 128:(hd + 1) * 128], glrT[:, c0:c0 + n], True, True, r=[('wup',), ('glr',)], w=[('ps', b)])
                        act(tmp[t][:, 0:n], psb[b][:, 0:n], AF.Sigmoid, r=[('ps', b), ('const',)], w=[('tmp', t)], bias=gbup_t[:, l, hd:hd + 1])
                        act(la[:, c0:c0 + n], tmp[t][:, 0:n], AF.Ln, r=[('tmp', t)], w=[('la',)])
                    def ev_v(mc, ci, c0, n, b):
                        cpy(vT[:, mc, c0:c0 + n], psb[b][:, 0:n], r=[('ps', b)], w=[('vT',)], eng='act')
                    proj(Bh, Bkey, KC, w_in[l, :, O_GV + hd * 256: O_GV + (hd + 1) * 256], 256, cts, ev_v)
                    def ev_g(mc, ci, c0, n, b):
                        act(sg[:, mc, c0:c0 + n], psb[b][:, 0:n], AF.Silu, r=[('ps', b)], w=[('sg',)])
                    proj(Bh, Bkey, KC, w_in[l, :, O_GR + hd * 256: O_GR + (hd + 1) * 256], 256, cts, ev_g)
                    recur(body, l, cts, qf, kf, la, vT, sg, 2, 128, 1.0 / 16.0, 128 ** -0.5, lambda vc: gln_t[:, l, vc:vc + 1], 16 + hd * 2,
                          None if body == 0 else sgl_d[l, hd],
                          sgl_d[l, hd] if body < nbody - 1 else None,
                          glp[l, hd] if body == nbody - 1 else None,
                          stgl[l, :, hd] if body == 0 else None,
                          gls[l, :, hd] if body == 0 else None, alC, alAt)
                    P.barrier()
                ck('gla')
                def Ykey(kc, ci):
                    return ('Y', kc, ci)
                for m2 in range(8):
                    for br in range(3):
                        wtg, wkg = wload(w_in[l, :, O_GATE + br * D + m2 * 256: O_GATE + br * D + (m2 + 1) * 256], KC, 256)
                        wtb, wkb = wload(w_br[l, br, :, m2 * 256:(m2 + 1) * 256], 8, 256)
                        for mc in range(2):
                            for ci, (c0, n) in enumerate(cts):
                                bg, bp = nps(), nps()
                                for kc in range(KC):
                                    mm(psb[bg][:, 0:n], wtg[:, kc, mc * 128:(mc + 1) * 128], Bh[:, kc, c0:c0 + n], kc == 0, kc == KC - 1,
                                       r=[wkg, Bkey(kc, ci)], w=[('ps', bg)])
                                for kc in range(8):
                                    mm(psb[bp][:, 0:n], wtb[:, kc, mc * 128:(mc + 1) * 128], Y[:, br * 8 + kc, c0:c0 + n], kc == 0, kc == 7,
                                       r=[wkb, Ykey(br * 8 + kc, ci)], w=[('ps', bp)])
                                t = ntmp()
                                act(tmp[t][:, 0:n], psb[bg][:, 0:n], AF.Sigmoid, r=[('ps', bg)], w=[('tmp', t)])
                                if br == 0:
                                    tt(acc[:, mc, c0:c0 + n], tmp[t][:, 0:n], psb[bp][:, 0:n], ALU.mult, r=[('tmp', t), ('ps', bp)], w=[('acc', mc, ci)])
                                else:
                                    tt(tmp[t][:, 0:n], tmp[t][:, 0:n], psb[bp][:, 0:n], ALU.mult, r=[('tmp', t), ('ps', bp)], w=[('tmp', t)])
                                    if br == 1:
                                        tt(acc[:, mc, c0:c0 + n], acc[:, mc, c0:c0 + n], tmp[t][:, 0:n], ALU.add, r=[('tmp', t), ('acc', mc, ci)], w=[('acc', mc, ci)])
                                    else:
                                        tt(Cm[:, m2 * 2 + mc, c0:c0 + n], acc[:, mc, c0:c0 + n], tmp[t][:, 0:n], ALU.add,
                                           r=[('tmp', t), ('acc', mc, ci)], w=[('C', m2 * 2 + mc, ci)])
                P.barrier()
                ck('merge')
                def Ckey(kc, ci):
                    return ('C', kc, ci)
                for u in range(8):
                    def ev(mc, ci, c0, n, b, u=u):
                        cpy(A[:, u * 2 + mc, c0:c0 + n], psb[b][:, 0:n], r=[('ps', b)], w=[Akey(u * 2 + mc, ci)], eng='act' if (mc + ci) % 2 else 'dve')
                    proj(Cm, Ckey, KC, w_out[l, :, u * 256:(u + 1) * 256], 256, cts, ev)
                resid_update(cts, gg1)
                ck('mixer')
                store_x(cts)
                norm_to_h(cts, gs2, None, 48)
                P.barrier()
                for gi in range(4):
                    for u in range(8):
                        def ev(mc, ci, c0, n, b, u=u):
                            t = ntmp()
                            act(tmp[t][:, 0:n], psb[b][:, 0:n], AF.Relu, r=[('ps', b)], w=[('tmp', t)])
                            tt(Cm[:, u * 2 + mc, c0:c0 + n], tmp[t][:, 0:n], tmp[t][:, 0:n], ALU.mult, r=[('tmp', t)], w=[('C', u * 2 + mc, ci)])
                        proj(Bh, Bkey, KC, w_up[l, :, gi * D + u * 256: gi * D + (u + 1) * 256], 256, cts, ev)
                    for u in range(8):
                        def ev(mc, ci, c0, n, b, u=u, gi=gi):
                            if gi == 0:
                                cpy(A[:, u * 2 + mc, c0:c0 + n], psb[b][:, 0:n], r=[('ps', b)], w=[Akey(u * 2 + mc, ci)], eng='act')
                            else:
                                tt(A[:, u * 2 + mc, c0:c0 + n], A[:, u * 2 + mc, c0:c0 + n], psb[b][:, 0:n], ALU.add,
                                   r=[('ps', b), Akey(u * 2 + mc, ci)], w=[Akey(u * 2 + mc, ci)])
                        proj(Cm, Ckey, KC, w_dn[l, gi * D:(gi + 1) * D, u * 256:(u + 1) * 256], 256, cts, ev)
                resid_update(cts, gg2)
                P.barrier()
            yo = [carve(arC, i * D * 4, 128, [D], F32) for i in range(2)]
            for blk in range(NPB // 128):
                q = blk % 2
                for k4 in range(0, KC, 4):
                    b = nps()
                    for j in range(4):
                        kc = k4 + j
                        tr(psb[b][:, j * 128:(j + 1) * 128], A[:, kc, blk * 128:(blk + 1) * 128], ident[:, :], r=[Akey(kc, blk // 4), ('const',)], w=[('ps', b)])
                    cpy(yo[q][:, k4 * 128:(k4 + 4) * 128], psb[b][:, :], r=[('ps', b)], w=[('yo', q)], eng='act' if (k4 // 4) % 2 else 'dve')
                dma('sp', [(yp[body * NPB + blk * 128: body * NPB + (blk + 1) * 128, :], yo[q][:, :])], r=[('yo', q)], w=[('ypout',)])
            if body == 0:
                for k4 in range(0, KC, 4):
                    b = nps()
                    for j in range(4):
                        kc = k4 + j
                        tr(psb[b][0:NS, j * 128:(j + 1) * 128], A[:, kc, NPB:NCOL], ident[:, :], r=[Akey(kc, 2), ('const',)], w=[('ps', b)])
                    cpy(yo[0][0:NS, k4 * 128:(k4 + 4) * 128], psb[b][0:NS, :], r=[('ps', b)], w=[('yo', 0)])
                dma('sp', [(ysm[:, :], yo[0][0:NS, :])], r=[('yo', 0)], w=[('ysmout',)])
            P.barrier()


    try:
        gen()
    except StopBuild:
        pass
    P.barrier()
    P.op('sp', lambda h: None, r=(), w=[('end',)])

    sems = {e: es.enter_context(nc.semaphore("s_" + e)) for e in ENG}
    dsems = {e: [es.enter_context(nc.semaphore("d_%s%d" % (e, i))) for i in range(NDSEM)] for e in ('sp', 'pool')}
    for e in ENG:
        dsems.setdefault(e, [None] * NDSEM)
    P.finalize(sems, dsems)
    with nc.Block() as block:
        @block.tensor
        def _(h):
            P.emit('pe', h, sems)

        @block.scalar
        def _(h):
            P.emit('act', h, sems)

        @block.vector
        def _(h):
            P.emit('dve', h, sems)

        @block.gpsimd
        def _(h):
            P.emit('pool', h, sems)

        @block.sync
        def _(h):
            P.emit('sp', h, sems)
    es.close()
    return nc, {e: len(P.ops[e]) for e in ENG}


def kernel(x_prompt, x_sample, state_hgrn, state_gla, c_prompt, c_sample, w_ada, b_ada, g_pre_mix, g_post_mix,
           g_pre_mlp, g_post_mlp, w_in, sgu_ln_g, sgu_ln_b, sgu_w_s, sgu_b_s, hg_lb, hg_norm_g, gla_w_up, gla_b_up,
           gla_norm_g, w_branch, w_out, w_mlp_up, w_mlp_down, _depth=L_FULL, _stop=None, _ncores=8):
    f = np.float32
    asc = lambda a: np.ascontiguousarray(np.asarray(a, dtype=f))
    Lq = L_FULL
    def fm_vec(v):
        return asc(np.asarray(v).reshape(Lq, KC, 128).transpose(2, 0, 1))
    gvec = asc(np.stack([fm_vec(g_pre_mix), fm_vec(g_post_mix), fm_vec(g_pre_mlp), fm_vec(g_post_mlp)], axis=1))
    bada = asc(np.asarray(b_ada).reshape(Lq, 96, 128).transpose(2, 0, 1))
    lng = asc(np.broadcast_to(np.asarray(sgu_ln_g)[:, None, :], (Lq, 128, BW)))
    lnb = asc(np.broadcast_to(np.asarray(sgu_ln_b)[:, None, :], (Lq, 128, BW)))
    wst = asc(np.asarray(sgu_w_s).transpose(0, 3, 1, 2))
    bsr = asc(np.broadcast_to(np.asarray(sgu_b_s)[:, None, :, :], (Lq, 128, 8, 128)))
    w00 = asc(np.broadcast_to(np.asarray(sgu_w_s)[:, None, :, 0, 0], (Lq, 16, 8)))
    hglb = asc(np.asarray(hg_lb).reshape(Lq, 8, 128).transpose(2, 1, 0))
    hgn = asc(np.asarray(hg_norm_g).T)
    gln = asc(np.asarray(gla_norm_g).reshape(Lq, 2, 128).transpose(2, 0, 1))
    gbup = asc(np.asarray(gla_b_up).reshape(Lq, 4, 128).transpose(2, 0, 1))
    c_ident = np.eye(128, dtype=f)
    c_tri = np.triu(np.ones((128, 128), dtype=f))
    dd = _depth
    shared = dict(w_ada=asc(w_ada[:dd]), w_in=asc(w_in[:dd]), w_br=asc(w_branch[:dd]), w_out=asc(w_out[:dd]), w_up=asc(w_mlp_up[:dd]),
                  w_dn=asc(w_mlp_down[:dd]), bada=bada, gvec=gvec, lng=lng, lnb=lnb, wst=wst, bsr=bsr, w00=w00, hglb=hglb,
                  hgn=hgn, gln=gln, gwup=asc(gla_w_up), gbup=gbup, c_ident=c_ident, c_tri=c_tri)
    xpn, xsn = np.asarray(x_prompt), np.asarray(x_sample)
    shn, sgn = np.asarray(state_hgrn), np.asarray(state_gla)
    cpn, csn = np.asarray(c_prompt), np.asarray(c_sample)
    in_maps = []
    for c in range(8):
        b = c % 4
        crows = np.concatenate([cpn[b:b + 1], csn[c * NS:(c + 1) * NS]], axis=0)
        cfm = asc(crows.reshape(17, KC, 128).transpose(2, 1, 0))
        m = dict(shared)
        m.update(xp=asc(xpn[b]), xsm=asc(xsn[c * NS:(c + 1) * NS, 0, :]), cfm=cfm,
                 sthg=asc(shn[:dd, c * NS:(c + 1) * NS]), stgl=asc(sgn[:dd, c * NS:(c + 1) * NS]))
        in_maps.append(m)
    nc, _ = build(depth=_depth, stop=_stop)
    res = run_bass_kernel_spmd(nc, in_maps[:_ncores], core_ids=list(range(_ncores)))
    R = res.results
    if _ncores < 8:
        return R
    y_prompt = np.stack([R[b]["yp"] for b in range(4)], axis=0).astype(f)
    y_sample = np.concatenate([R[c]["ysm"] for c in range(8)], axis=0).reshape(128, 1, D).astype(f)
    hgrn_prompt = np.stack([R[b]["hgp"] for b in range(4)], axis=1).astype(f)
    gla_prompt = np.stack([R[b]["glp"] for b in range(4)], axis=1).astype(f)
    hgrn_sample = np.concatenate([R[c]["hgs"] for c in range(8)], axis=1).astype(f)
    gla_sample = np.concatenate([R[c]["gls"] for c in range(8)], axis=1).astype(f)
    sgu_v = np.concatenate([R[c]["sguv"] for c in range(8)], axis=1).reshape(Lq, 128, 1, BW).astype(f)
    return (y_prompt, y_sample, hgrn_prompt, gla_prompt, hgrn_sample, gla_sample, sgu_v)
```

```python
import numpy as np
from contextlib import ExitStack
import concourse.bass as bass
import concourse.mybir as mybir
from concourse.bass_utils import run_bass_kernel_spmd

F32 = mybir.dt.float32
BF16 = mybir.dt.bfloat16
AF = mybir.ActivationFunctionType
ALU = mybir.AluOpType
AX = mybir.AxisListType

D = 2048
KC = 16
L_FULL = 4
NPB = 1024
NS = 16
NCOL = NPB + NS
SEQ = 2048
BW = 1024
INW = 15376
O_U, O_V, O_HQ, O_HF, O_HI, O_HG, O_GQ, O_GK, O_GV, O_GR, O_GLR, O_GATE = (
    0, 1024, 2048, 3072, 4096, 5120, 6144, 6656, 7168, 8192, 9216, 9232)
EPS = 1e-6
ENG = ['pe', 'act', 'dve', 'pool', 'sp']
NDSEM = 8


class Prog:
    def __init__(self):
        self.ops = {e: [] for e in ENG}
        self.lastw = {}
        self.rd = {}
        self.floor = set()

    def op(self, eng, fn, r=(), w=(), dma=0, nofloor=False):
        idx = len(self.ops[eng])
        me = (eng, idx)
        deps = set()
        for k in r:
            t = self.lastw.get(k)
            if t is not None:
                deps.add(t)
        for k in w:
            t = self.lastw.get(k)
            if t is not None:
                deps.add(t)
            for t2 in self.rd.get(k, ()):
                deps.add(t2)
        if not nofloor:
            deps |= self.floor
        deps.discard(me)
        for k in r:
            self.rd.setdefault(k, []).append(me)
        for k in w:
            self.lastw[k] = me
            self.rd[k] = []
        self.ops[eng].append({'fn': fn, 'deps': deps, 'dma': dma, 'sig': False})
        return me

    def barrier(self):
        fl = set()
        for e in ENG:
            n = len(self.ops[e])
            if n == 0:
                continue
            if e in ('sp', 'pool'):
                for i in range(max(0, n - NDSEM), n):
                    fl.add((e, i))
            else:
                fl.add((e, n - 1))
        self.floor = fl

    def finalize(self, sems, dsems):
        ops = self.ops
        for e in ENG:
            for rec in ops[e]:
                for (f, i) in rec['deps']:
                    ops[f][i]['sig'] = True
        for e in ENG:
            cnt = 0
            dcum = [0] * NDSEM
            di = 0
            for rec in ops[e]:
                if rec['dma']:
                    j = di % NDSEM
                    rec['prev'] = (dsems[e][j], dcum[j])
                    dcum[j] += 16 * rec['dma']
                    rec['tok'] = (dsems[e][j], dcum[j])
                    di += 1
                else:
                    if rec['sig']:
                        cnt += 1
                    rec['tok'] = (sems[e], cnt)
        for e in ENG:
            known = {}
            for rec in ops[e]:
                waits = {}
                for (f, i) in rec['deps']:
                    prod = ops[f][i]
                    if (not prod['dma']) and f == e and e == 'pe':
                        continue
                    sem, val = prod['tok']
                    key = id(sem)
                    if known.get(key, 0) >= val:
                        continue
                    if key not in waits or waits[key][1] < val:
                        waits[key] = (sem, val)
                if rec['dma']:
                    sem, val = rec['prev']
                    key = id(sem)
                    if val > 0 and known.get(key, 0) < val:
                        if key not in waits or waits[key][1] < val:
                            waits[key] = (sem, val)
                for key, (sem, val) in waits.items():
                    known[key] = val
                rec['waits'] = list(waits.values())

    def emit(self, e, h, sems):
        for rec in self.ops[e]:
            for (sem, val) in rec['waits']:
                h.wait_ge(sem, val)
            ins = rec['fn'](h)
            if rec['dma']:
                assert len(ins) == rec['dma']
                for i_ in ins:
                    i_.then_inc(rec['tok'][0], 16)
            elif rec['sig']:
                assert ins is not None
                ins.then_inc(sems[e], 1)


class StopBuild(Exception):
    pass


def build(depth=L_FULL, nbody=2, stop=None):
    LW = depth
    nc = bass.Bass("TRN2", target_bir_lowering=False)
    P = Prog()
    es = ExitStack()

    def ck(name):
        if stop == name:
            raise StopBuild()

    def din(name, shape):
        return nc.dram_tensor(name, list(shape), F32, kind="ExternalInput").ap()

    def dout(name, shape):
        return nc.dram_tensor(name, list(shape), F32, kind="ExternalOutput").ap()

    def dscr(name, shape):
        return nc.dram_tensor(name, list(shape), F32, kind="Internal").ap()

    xp = din("xp", [SEQ, D])
    xsm = din("xsm", [NS, D])
    cfm = din("cfm", [128, KC, 17])
    sthg = din("sthg", [LW, NS, 8, 128, 128])
    stgl = din("stgl", [LW, NS, 4, 128, 256])
    w_ada = din("w_ada", [LW, D, 6 * D])
    w_in = din("w_in", [LW, D, INW])
    w_br = din("w_br", [LW, 3, BW, D])
    w_out = din("w_out", [LW, D, D])
    w_up = din("w_up", [LW, D, 4 * D])
    w_dn = din("w_dn", [LW, 4 * D, D])
    bada = din("bada", [128, L_FULL, 96])
    gvec = din("gvec", [128, 4, L_FULL, KC])
    lng = din("lng", [L_FULL, 128, BW])
    lnb = din("lnb", [L_FULL, 128, BW])
    wst = din("wst", [L_FULL, 128, 8, 128])
    bsr = din("bsr", [L_FULL, 128, 8, 128])
    w00 = din("w00", [L_FULL, 16, 8])
    hglb = din("hglb", [128, 8, L_FULL])
    hgn = din("hgn", [128, L_FULL])
    gln = din("gln", [128, L_FULL, 2])
    gwup = din("gwup", [L_FULL, 16, 512])
    gbup = din("gbup", [128, L_FULL, 4])
    c_ident = din("c_ident", [128, 128])
    c_tri = din("c_tri", [128, 128])

    yp = dout("yp", [SEQ, D])
    ysm = dout("ysm", [NS, D])
    hgp = dout("hgp", [L_FULL, 8, 128, 128])
    glp = dout("glp", [L_FULL, 4, 128, 256])
    hgs = dout("hgs", [L_FULL, NS, 8, 128, 128])
    gls = dout("gls", [L_FULL, NS, 4, 128, 256])
    sguv = dout("sguv", [L_FULL, NS, BW])

    xs_d = dscr("xs_d", [128, KC * NCOL])
    mod_d = dscr("mod_d", [L_FULL, 128, 96 * 17])
    shg_d = dscr("shg_d", [L_FULL, 8, 128, 128])
    sgl_d = dscr("sgl_d", [L_FULL, 4, 128, 256])

    A_B, B_B, C_B, W_B = KC * NCOL * 4, KC * NCOL * 2, KC * NCOL * 2, 3 * KC * 256 * 2
    T_B = 54000
    arA = es.enter_context(nc.sbuf_tensor("arA", [128, A_B // 4], F32))
    arB = es.enter_context(nc.sbuf_tensor("arB", [128, B_B // 4], F32))
    arC = es.enter_context(nc.sbuf_tensor("arC", [128, C_B // 4], F32))
    arW = es.enter_context(nc.sbuf_tensor("arW", [128, W_B // 4], F32))
    arT = es.enter_context(nc.sbuf_tensor("arT", [128, T_B // 4], F32))
    psb = [es.enter_context(nc.psum_tensor("ps%d" % i, [128, 512], F32)) for i in range(8)]

    def carve(ar, off, parts, shape, dt):
        n = int(np.prod(shape))
        esz = 4 if dt == F32 else 2
        assert off % 4 == 0 and (n * esz) % 4 == 0
        v = ar[0:parts, off // 4: (off + n * esz) // 4]
        if dt != F32:
            v = v.bitcast(dt)
        if len(shape) == 2:
            v = v.rearrange("p (a b) -> p a b", b=shape[1])
        elif len(shape) == 3:
            v = v.rearrange("p (a b c) -> p a b c", b=shape[1], c=shape[2])
        return v

    class Alloc:
        def __init__(self, ar, base, size):
            self.ar, self.base, self.size, self.off = ar, base, size, 0

        def get(self, parts, shape, dt):
            n = int(np.prod(shape)) * (4 if dt == F32 else 2)
            n4 = (n + 3) // 4 * 4
            assert self.off + n4 <= self.size, ("arena overflow", self.off, n4, self.size)
            v = carve(self.ar, self.base + self.off, parts, shape, dt)
            self.off += n4
            return v

        def reset(self):
            self.off = 0

    A = carve(arA, 0, 128, [KC, NCOL], F32)
    Y = carve(arA, 0, 128, [24, NCOL], BF16)
    Bh = carve(arB, 0, 128, [KC, NCOL], BF16)
    Cm = carve(arC, 0, 128, [KC, NCOL], BF16)
    Wb = [carve(arW, s * KC * 256 * 2, 128, [KC, 256], BF16) for s in range(3)]
    alC = Alloc(arC, 0, C_B)
    alAt = Alloc(arA, 24 * NCOL * 2, A_B - 24 * NCOL * 2 - NCOL * 4)
    alT = Alloc(arT, 0, T_B)

    ident = alT.get(128, [128], F32)
    identb = alT.get(128, [128], BF16)
    onesb = alT.get(128, [128], BF16)
    tri = alT.get(128, [128], F32)
    gv_t = alT.get(128, [4, L_FULL, KC], F32)
    bada_t = alT.get(128, [L_FULL, 96], F32)
    lb_t = alT.get(128, [8, L_FULL], F32)
    oml_t = alT.get(128, [8, L_FULL], F32)
    noml_t = alT.get(128, [8, L_FULL], F32)
    hgn_t = alT.get(128, [L_FULL], F32)
    gln_t = alT.get(128, [L_FULL, 2], F32)
    gbup_t = alT.get(128, [L_FULL, 4], F32)
    mod_t = alT.get(128, [96, 17], F32)
    gs1 = alT.get(128, [KC, 17], F32)
    gg1 = alT.get(128, [KC, 17], F32)
    gs2 = alT.get(128, [KC, 17], F32)
    gg2 = alT.get(128, [KC, 17], F32)
    tmp_off = alT.off
    tmp = [alT.get(128, [512], F32) for _ in range(4)]
    St = carve(arT, tmp_off, 128, [2048], F32)
    xt = [alT.get(128, [512], F32) for _ in range(2)]
    sqb = [alT.get(128, [512], BF16) for _ in range(2)]
    rs = alT.get(128, [NCOL], F32)
    S_t = alT.get(128, [512], F32)
    csc = alT.get(128, [5, 32], F32)
    attsb2 = [alT.get(128, [512], BF16) for _ in range(2)]
    oT = alT.get(128, [2, NCOL], F32)
    acc = oT
    kS = alT.get(16, [128], F32)
    vS = alT.get(16, [256], F32)
    ksc = alT.get(128, [16], F32)
    vSf = alT.get(128, [2, 16], F32)
    smp = alT.get(128, [16 * 8], F32)
    wup_t = alT.get(16, [512], F32)
    glrT = carve(arA, A_B - NCOL * 4, 16, [NCOL], F32)
    T_PERM = alT.off

    cnt = {'ps': 0, 'tmp': 0, 'xt': 0, 'sq': 0, 'w': 0, 'nb': 8}

    def nps():
        i = cnt['ps'] % cnt['nb']
        cnt['ps'] += 1
        return i

    def ntmp():
        i = cnt['tmp'] % 4
        cnt['tmp'] += 1
        return i

    def dma(eng, outs_ins, r=(), w=(), nofloor=False):
        def fn(h, oi=outs_ins):
            return [h.dma_start(out=o, in_=i) for (o, i) in oi]
        return P.op(eng, fn, r=r, w=w, dma=len(outs_ins), nofloor=nofloor)

    def act(out, in_, func, r, w, bias=None, scale=None):
        kw = {}
        if bias is not None:
            kw['bias'] = bias
        if scale is not None:
            kw['scale'] = scale
        return P.op('act', lambda h, o=out, i=in_, f=func, kw=kw: h.activation(out=o, in_=i, func=f, **kw), r=r, w=w)

    def tt(out, in0, in1, op, r, w):
        return P.op('dve', lambda h, o=out, a=in0, b=in1, op=op: h.tensor_tensor(out=o, in0=a, in1=b, op=op), r=r, w=w)

    def ts(out, in0, s1, s2, op0, op1, r, w):
        return P.op('dve', lambda h, o=out, a=in0, s1=s1, s2=s2, op0=op0, op1=op1:
                    h.tensor_scalar(out=o, in0=a, scalar1=s1, scalar2=s2, op0=op0, op1=op1), r=r, w=w)

    def stt(out, in0, sc, in1, op0, op1, r, w):
        return P.op('dve', lambda h, o=out, a=in0, s=sc, b=in1, op0=op0, op1=op1:
                    h.scalar_tensor_tensor(out=o, in0=a, scalar=s, in1=b, op0=op0, op1=op1), r=r, w=w)

    def cpy(out, in_, r, w, eng='dve'):
        if eng == 'dve':
            return P.op('dve', lambda h, o=out, i=in_: h.tensor_copy(out=o, in_=i), r=r, w=w)
        return act(out, in_, AF.Copy, r, w)

    def recip(out, in_, r, w):
        return P.op('dve', lambda h, o=out, i=in_: h.reciprocal(out=o, in_=i), r=r, w=w)

    def mm(out, lhsT, rhs, start, stop, r, w):
        return P.op('pe', lambda h, o=out, a=lhsT, b=rhs, st=start, sp=stop:
                    h.matmul(o, lhsT=a, rhs=b, start=st, stop=sp), r=r, w=w)

    def tr(out, in_, idn, r, w):
        return P.op('pe', lambda h, o=out, i=in_, d=idn: h.transpose(out=o, in_=i, identity=d), r=r, w=w)

    def rsqrt_inplace(t_ap, src_ap, scale, r, w):
        act(t_ap, src_ap, AF.Sqrt, r=r, w=w, bias=EPS, scale=scale)
        recip(t_ap, t_ap, r=w, w=w)

    def wload(src2d, nkc, ncols):
        i = cnt['w']
        cnt['w'] += 1
        s = i % 3
        dst = Wb[s][:, 0:nkc, 0:ncols]
        srcv = src2d.rearrange("(kc p) c -> p kc c", p=128)
        P.op('pool', lambda h, o=dst, i_=srcv: [h.dma_start(out=o, in_=i_)], w=[('w', s)], dma=1, nofloor=True)
        return Wb[s], ('w', s)

    def gen():
        dma('sp', [(ident[:, :], c_ident[:, :]), (tri[:, :], c_tri[:, :]),
                   (gv_t[:, :, :, :], gvec[:, :, :, :]), (bada_t[:, :, :], bada[:, :, :]),
                   (lb_t[:, :, :], hglb[:, :, :]), (hgn_t[:, :], hgn[:, :]), (gln_t[:, :, :], gln[:, :, :]),
                   (gbup_t[:, :, :], gbup[:, :, :])], w=[('const',)])
        cpy(identb[:, :], ident[:, :], r=[('const',)], w=[('identb',)])
        P.op('dve', lambda h: h.memset(onesb[:, :], 1.0), w=[('onesb',)])
        lbx = csc[:, 0, 0:8]
        P.op('dve', lambda h: h.tensor_reduce(out=lbx, in_=lb_t[:, :, :], axis=AX.X, op=ALU.max), r=[('const',)], w=[('lbx',)])
        tt(lb_t[:, :, :], lb_t[:, :, :], lbx.unsqueeze(2).to_broadcast([128, 8, L_FULL]), ALU.subtract, r=[('const',), ('lbx',)], w=[('lb',)])
        act(lb_t[:, :, :], lb_t[:, :, :], AF.Exp, r=[('lb',)], w=[('lb',)])
        P.op('dve', lambda h: h.tensor_reduce(out=lbx, in_=lb_t[:, :, :], axis=AX.X, op=ALU.add), r=[('lb',)], w=[('lbx',)])
        recip(lbx, lbx, r=[('lbx',)], w=[('lbx',)])
        tt(lb_t[:, :, :], lb_t[:, :, :], lbx.unsqueeze(2).to_broadcast([128, 8, L_FULL]), ALU.mult, r=[('lb',), ('lbx',)], w=[('lb',)])
        P.op('dve', lambda h: h.memset(lb_t[:, :, 0:1], 0.0), r=[('lb',)], w=[('lb',)])
        for l_ in range(2, L_FULL):
            tt(lb_t[:, :, l_:l_ + 1], lb_t[:, :, l_:l_ + 1], lb_t[:, :, l_ - 1:l_], ALU.add, r=[('lb',)], w=[('lb',)])
        ts(oml_t[:, :, :], lb_t[:, :, :], -1.0, 1.0, ALU.mult, ALU.add, r=[('lb',)], w=[('oml',)])
        ts(noml_t[:, :, :], oml_t[:, :, :], -1.0, None, ALU.mult, ALU.bypass, r=[('oml',)], w=[('noml',)])

        cs_f = carve(arC, 0, 128, [KC, 17], F32)
        cs_b = carve(arC, KC * 17 * 4, 128, [KC, 17], BF16)
        dma('sp', [(cs_f[:, :, :], cfm[:, :, :])], w=[('csf',)])
        act(cs_b[:, :, :], cs_f[:, :, :], AF.Silu, r=[('csf',)], w=[('csb',)])
        for l in range(depth):
            for u in range(48):
                wt, wk = wload(w_ada[l, :, u * 256:(u + 1) * 256], KC, 256)
                for mc in range(2):
                    b = nps()
                    for kc in range(KC):
                        mm(psb[b][:, 0:17], wt[:, kc, mc * 128:(mc + 1) * 128], cs_b[:, kc, :], kc == 0, kc == KC - 1,
                           r=[wk, ('csb',)], w=[('ps', b)])
                    ch = u * 2 + mc
                    act(mod_t[:, ch, :], psb[b][:, 0:17], AF.Identity, r=[('ps', b), ('const',)], w=[('mod',)],
                        bias=bada_t[:, l, ch:ch + 1])
            dma('sp', [(mod_d[l].rearrange("p (a b) -> p a b", b=17), mod_t[:, :, :])], r=[('mod',)], w=[('modd', l)])
        P.barrier()
        ck('pro')

        def col_tiles(body):
            cts = [(0, 512), (512, 512)]
            if body == 0:
                cts.append((NPB, NS))
            return cts

        def Akey(kc, ci):
            return ('A', kc, ci)

        def stats_rs(src, cts, keyf, nfeat_chunks, scale):
            for ci, (c0, n) in enumerate(cts):
                b = nps()
                for kc in range(nfeat_chunks):
                    q = cnt['sq'] % 2
                    cnt['sq'] += 1
                    act(sqb[q][:, 0:n], src[:, kc, c0:c0 + n], AF.Square, r=[keyf(kc, ci)], w=[('sq', q)])
                    mm(psb[b][:, 0:n], onesb[:, :], sqb[q][:, 0:n], kc == 0, kc == nfeat_chunks - 1,
                       r=[('sq', q), ('onesb',)], w=[('ps', b)])
                act(rs[:, c0:c0 + n], psb[b][:, 0:n], AF.Sqrt, r=[('ps', b)], w=[('rs', ci)], bias=EPS, scale=scale)
                recip(rs[:, c0:c0 + n], rs[:, c0:c0 + n], r=[('rs', ci)], w=[('rs', ci)])

        def norm_to_h(cts, GS, SH, shoff):
            stats_rs(A, cts, Akey, KC, 1.0 / D)
            for ci, (c0, n) in enumerate(cts):
                for kc in range(KC):
                    t = ntmp()
                    tt(tmp[t][:, 0:n], A[:, kc, c0:c0 + n], rs[:, c0:c0 + n], ALU.mult,
                       r=[Akey(kc, ci), ('rs', ci)], w=[('tmp', t)])
                    if n == 512:
                        act(Bh[:, kc, c0:c0 + n], tmp[t][:, 0:n], AF.Identity, r=[('tmp', t), ('gsh',)], w=[('B', kc, ci)],
                            bias=mod_t[:, shoff + kc, 0:1], scale=GS[:, kc, 0:1])
                    else:
                        tt(tmp[t][:, 0:n], tmp[t][:, 0:n], GS[:, kc, 1:17], ALU.mult, r=[('tmp', t), ('gsh',)], w=[('tmp', t)])
                        tt(Bh[:, kc, c0:c0 + n], tmp[t][:, 0:n], mod_t[:, shoff + kc, 1:17], ALU.add,
                           r=[('tmp', t), ('gsh',)], w=[('B', kc, ci)])

        def store_x(cts):
            ncb_ = cts[-1][0] + cts[-1][1]
            dma('sp', [(xs_d[:, kc * NCOL: kc * NCOL + ncb_], A[:, kc, 0:ncb_]) for kc in range(KC)],
                r=[Akey(kc, ci) for kc in range(KC) for ci in range(len(cts))], w=[('xs',)])

        def resid_update(cts, GG):
            stats_rs(A, cts, Akey, KC, 1.0 / D)
            for ci, (c0, n) in enumerate(cts):
                for kc in range(KC):
                    q = cnt['xt'] % 2
                    cnt['xt'] += 1
                    dma('sp', [(xt[q][:, 0:n], xs_d[:, kc * NCOL + c0: kc * NCOL + c0 + n])], r=[('xs',)], w=[('xt', q)])
                    t = ntmp()
                    tt(tmp[t][:, 0:n], A[:, kc, c0:c0 + n], rs[:, c0:c0 + n], ALU.mult,
                       r=[Akey(kc, ci), ('rs', ci)], w=[('tmp', t)])
                    if n == 512:
                        stt(A[:, kc, c0:c0 + n], tmp[t][:, 0:n], GG[:, kc, 0:1], xt[q][:, 0:n], ALU.mult, ALU.add,
                            r=[('tmp', t), ('xt', q), ('gsh',)], w=[Akey(kc, ci)])
                    else:
                        tt(tmp[t][:, 0:n], tmp[t][:, 0:n], GG[:, kc, 1:17], ALU.mult, r=[('tmp', t), ('gsh',)], w=[('tmp', t)])
                        tt(A[:, kc, c0:c0 + n], tmp[t][:, 0:n], xt[q][:, 0:n], ALU.add,
                           r=[('tmp', t), ('xt', q)], w=[Akey(kc, ci)])

        def proj(src, srckey, nkc, wsrc2d, ncols, cts, evac):
            wt, wk = wload(wsrc2d, nkc, ncols)
            nmc = (ncols + 127) // 128
            for mc in range(nmc):
                mw = min(128, ncols - mc * 128)
                for ci, (c0, n) in enumerate(cts):
                    b = nps()
                    for kc in range(nkc):
                        mm(psb[b][0:mw, 0:n], wt[:, kc, mc * 128:mc * 128 + mw], src[:, kc, c0:c0 + n], kc == 0, kc == nkc - 1,
                           r=[wk, srckey(kc, ci)], w=[('ps', b)])
                    evac(mc, ci, c0, n, b)

        def Bkey(kc, ci):
            return ('B', kc, ci)

        def gelu_evac(dst, ps_ap, parts, n, b, wkeys):
            t1, t2 = ntmp(), ntmp()
            a, c = tmp[t1][0:parts, 0:n], tmp[t2][0:parts, 0:n]
            act(a, ps_ap, AF.Square, r=[('ps', b)], w=[('tmp', t1)])
            ts(a, a, 0.044715, 1.0, ALU.mult, ALU.add, r=[('tmp', t1)], w=[('tmp', t1)])
            tt(a, a, ps_ap, ALU.mult, r=[('tmp', t1), ('ps', b)], w=[('tmp', t1)])
            act(c, a, AF.Sigmoid, r=[('tmp', t1)], w=[('tmp', t2)], scale=1.5957691216057308)
            tt(dst, c, ps_ap, ALU.mult, r=[('tmp', t2), ('ps', b)], w=wkeys)

        def recur(body, l, cts, qf, kf, la, vT, sg, nvc, C, sc_l, kscale, gn_ap, ybase, st_in, st_scr, st_out, smp_in, smp_out, al, alt, mid=None):
            P.barrier()
            cnt['nb'] = 2
            dv = nvc * 128
            NCH = NPB // C
            half = C // 2
            off_rel = al.off
            rel = al.get(128, [NPB], F32)
            Ee = al.get(128, [NPB], F32)
            vdg = carve(al.ar, al.base + off_rel, 16, [2048], F32)
            Qt = al.get(128, [NPB], BF16)
            Kt = al.get(128, [NPB], BF16)
            Ktm = al.get(C, [NCH, 128], BF16)
            Vtm = alt.get(C, [NCH, dv], BF16)
            Bs, rho, sig_, gam, scr = (csc[:, i, 0:NCH] for i in range(5))
            P.op('dve', lambda h: h.memset(rel[:, :], 1.0), w=[('rel',)])
            Bc = Ee
            P.op('dve', lambda h, o=Bc[:, :], d0=rel[:, :], d1=la[:, 0:NPB]: h.tensor_tensor_scan(out=o, data0=d0, data1=d1, initial=0.0, op0=ALU.mult, op1=ALU.add),
                 r=[('la',), ('rel',)], w=[('Ee',)])
            Bc3 = Bc[:, :].rearrange("p (c j) -> p c j", j=C)
            bref = Bc3[:, :, half - 1:half]
            bend = Bc3[:, :, C - 1:C]
            tt(rel[:, :].rearrange("p (c j) -> p c j", j=C), Bc3, bref.to_broadcast([128, NCH, C]), ALU.subtract, r=[('Ee',)], w=[('rel',)])
            P.op('dve', lambda h, o=Bs: h.memset(o[:, 0:1], 0.0), w=[('Bs',)])
            if NCH > 1:
                cpy(Bs[:, 1:NCH].unsqueeze(2), bend[:, 0:NCH - 1, :], r=[('Ee',)], w=[('Bs',)])
            tt(scr.unsqueeze(2), bref, Bs.unsqueeze(2), ALU.subtract, r=[('Ee',), ('Bs',)], w=[('scr',)])
            act(rho, scr, AF.Exp, r=[('scr',)], w=[('rho',)], scale=sc_l)
            tt(scr.unsqueeze(2), bend, bref, ALU.subtract, r=[('Ee',), ('rho',)], w=[('scr',)])
            act(sig_, scr, AF.Exp, r=[('scr',)], w=[('sig',)], scale=sc_l)
            tt(scr.unsqueeze(2), bend, Bs.unsqueeze(2), ALU.subtract, r=[('Ee',), ('Bs',), ('sig',)], w=[('scr',)])
            act(gam, scr, AF.Exp, r=[('scr',)], w=[('gam',)], scale=sc_l)
            Qh = la[:, 0:NPB]
            Kh = carve(al.ar, al.base + off_rel + NPB * 4, 128, [NPB], BF16)
            act(Ee[:, :], rel[:, :], AF.Exp, r=[('rel',), ('gam',)], w=[('Ee',)], scale=sc_l)
            tt(Qt[:, :], qf[:, 0:NPB], Ee[:, :], ALU.mult, r=[('qf',), ('Ee',)], w=[('Qt',)])
            tt(Qh, qf[:, 0:NPB], Ee[:, :], ALU.mult, r=[('qf',), ('Ee',)], w=[('la',)])
            tt(Qh.rearrange("p (c j) -> p c j", j=C), Qh.rearrange("p (c j) -> p c j", j=C), rho.unsqueeze(2).to_broadcast([128, NCH, C]), ALU.mult,
               r=[('la',), ('rho',)], w=[('la',)])
            act(Ee[:, :], rel[:, :], AF.Exp, r=[('rel',), ('Qt',), ('la',)], w=[('Ee',)], scale=-sc_l)
            stt(Kt[:, :], kf[:, 0:NPB], kscale, Ee[:, :], ALU.mult, ALU.mult, r=[('kf',), ('Ee',)], w=[('Kt',)])
            tt(Kh[:, :].rearrange("p (c j) -> p c j", j=C), Kt[:, :].rearrange("p (c j) -> p c j", j=C), sig_.unsqueeze(2).to_broadcast([128, NCH, C]), ALU.mult,
               r=[('Kt',), ('sig',)], w=[('Ee',)])

            if mid is not None:
                cnt['nb'] = 8
                mid()
                cnt['nb'] = 2
            def tm_transposes(srcf, dst, width, skey, dkey):
                items = [(c, vc) for c in range(NCH) for vc in range(width // 128)]
                dflat = dst.rearrange("p c v -> p (c v)")
                for i0 in range(0, len(items), 8):
                    b = nps()
                    pv = psb[b][:, :].bitcast(BF16)
                    grp = items[i0:i0 + 8]
                    for j, (c, vc) in enumerate(grp):
                        tr(pv[0:C, j * 128:(j + 1) * 128], srcf(c, vc), identb[:, :], r=[('identb',), skey], w=[('ps', b)])
                    cpy(dflat[:, i0 * 128:(i0 + len(grp)) * 128], pv[0:C, 0:len(grp) * 128], r=[('ps', b)], w=[dkey],
                        eng='act' if (i0 // 8) % 2 else 'dve')
            tm_transposes(lambda c, vc: Kh[:, c * C:(c + 1) * C], Ktm, 128, ('Ee',), ('Ktm',))
            tm_transposes(lambda c, vc: vT[:, vc, c * C:(c + 1) * C], Vtm, dv, ('vT',), ('Vtm',))
            ck('r2')
            S2 = [S_t[:, 0:dv], S_t[:, 256:256 + dv]]
            if st_in is None:
                P.op('dve', lambda h, o=S2[1]: h.memset(o, 0.0), w=[('S', 1)])
            else:
                dma('sp', [(S2[1], st_in)], r=[('sscr', l)], w=[('S', 1)])
            UB = [5, 6, 7]
            def issue_U(c):
                u = UB[c % 3]
                mm(psb[u][:, 0:dv], Ktm[:, c, :], Vtm[:, c, :], True, True, r=[('Ktm',), ('Vtm',)], w=[('ps', u)])
            for c in range(min(2, NCH)):
                issue_U(c)
            cpb = 512 // C
            for gi_, c0_ in enumerate(range(0, NCH, cpb)):
                nchb = min(cpb, NCH - c0_)
                b = 2
                ab = attsb2[gi_ % 2]
                for j in range(nchb):
                    c = c0_ + j
                    mm(psb[b][0:C, j * C:(j + 1) * C], Kt[:, c * C:(c + 1) * C], Qt[:, c * C:(c + 1) * C], True, True,
                       r=[('Kt',), ('Qt',)], w=[('ps', b)])
                tt(ab[0:C, 0:nchb * C].rearrange("p (a b) -> p a b", b=C), psb[b][0:C, 0:nchb * C].rearrange("p (a b) -> p a b", b=C),
                   tri[0:C, 0:C].unsqueeze(1).to_broadcast([C, nchb, C]), ALU.mult, r=[('ps', b), ('const',)], w=[('att', gi_ % 2)])
                bo = [3, 4][0:nvc]
                for j in range(nchb):
                    c = c0_ + j
                    col = j * C
                    if c + 2 < NCH:
                        issue_U(c + 2)
                    for vc in range(nvc):
                        mm(psb[bo[vc]][:, col:col + C], Vtm[:, c, vc * 128:(vc + 1) * 128], ab[0:C, j * C:(j + 1) * C], True, False,
                           r=[('Vtm',), ('att', gi_ % 2)], w=[('ps', bo[vc])])
                        mm(psb[bo[vc]][:, col:col + C], S2[(c - 1) % 2][:, vc * 128:(vc + 1) * 128], Qh[:, c * C:(c + 1) * C], False, True,
                           r=[('S', (c - 1) % 2), ('la',)], w=[('ps', bo[vc])])
                    u = UB[c % 3]
                    stt(S2[c % 2], S2[(c - 1) % 2], gam[:, c:c + 1], psb[u][:, 0:dv], ALU.mult, ALU.add,
                        r=[('ps', u), ('S', (c - 1) % 2), ('gam',)], w=[('S', c % 2)])
                for vc in range(nvc):
                    cpy(oT[:, vc, c0_ * C:(c0_ + nchb) * C], psb[bo[vc]][:, 0:nchb * C], r=[('ps', bo[vc])], w=[('oT',)], eng='act')
            Sfin = S2[(NCH - 1) % 2]
            SK = ('S', (NCH - 1) % 2)
            ck('r3')
            if st_scr is not None:
                dma('sp', [(st_scr, Sfin)], r=[SK], w=[('sscr', l)])
            if st_out is not None:
                dma('sp', [(st_out, Sfin)], r=[SK], w=[('stout',)])
            if smp_in is not None:
                P.barrier()
                nsb = 2048 // dv
                fS = smp[:, 0:16]
                act(fS, la[:, NPB:NCOL], AF.Exp, r=[('la',)], w=[('fS',)], scale=sc_l)
                ts(ksc[:, :], kf[:, NPB:NCOL], kscale, None, ALU.mult, ALU.bypass, r=[('kf',)], w=[('ksc',)])
                b = nps()
                tr(psb[b][0:16, 0:128], ksc[:, :], ident[:, :], r=[('ksc',), ('const',)], w=[('ps', b)])
                cpy(kS[:, :], psb[b][0:16, 0:128], r=[('ps', b)], w=[('kS',)])
                cpy(vSf[:, 0:nvc, :], vT[:, :, NPB:NCOL], r=[('vT',)], w=[('vSf',)])
                b = nps()
                for vc in range(nvc):
                    tr(psb[b][0:16, vc * 128:(vc + 1) * 128], vSf[:, vc, :], ident[:, :], r=[('vSf',), ('const',)], w=[('ps', b)])
                cpy(vS[:, 0:dv], psb[b][0:16, 0:dv], r=[('ps', b)], w=[('vS',)])
                for sb_ in range(NS // nsb):
                    j0 = sb_ * nsb
                    St3 = St[:, 0:nsb * dv].rearrange("p (j v) -> p j v", v=dv)
                    dma('sp', [(St3, smp_in[j0:j0 + nsb].rearrange("j d v -> d j v"))], w=[('St',)])
                    tt(St3, St3, fS[:, j0:j0 + nsb].unsqueeze(2).to_broadcast([128, nsb, dv]), ALU.mult, r=[('St',), ('fS',)], w=[('St',)])
                    tt(vdg[:, 0:nsb * dv].rearrange("p (j v) -> p j v", v=dv), vS[:, 0:dv].unsqueeze(1).to_broadcast([16, nsb, dv]),
                       ident[0:16, j0:j0 + nsb].unsqueeze(2).to_broadcast([16, nsb, dv]), ALU.mult, r=[('vS',), ('const',), ('Ee',), ('rel',)], w=[('vdg',)])
                    for q0 in range(0, nsb * dv, 512):
                        b = nps()
                        mm(psb[b][:, :], kS[:, :], vdg[:, q0:q0 + 512], True, True, r=[('kS',), ('vdg',)], w=[('ps', b)])
                        tt(St[:, q0:q0 + 512], St[:, q0:q0 + 512], psb[b][:, :], ALU.add, r=[('St',), ('ps', b)], w=[('St',)])
                    dma('sp', [(smp_out[j0:j0 + nsb].rearrange("j d v -> d j v"), St3)], r=[('St',)], w=[('smpout',)])
                    for vc in range(nvc):
                        b = nps()
                        for j in range(nsb):
                            mm(psb[b][:, j:j + 1], St3[:, j, vc * 128:(vc + 1) * 128], qf[:, NPB + j0 + j:NPB + j0 + j + 1], True, True,
                               r=[('St',), ('qf',)], w=[('ps', b)])
                        cpy(oT[:, vc, NPB + j0:NPB + j0 + nsb], psb[b][:, 0:nsb], r=[('ps', b)], w=[('oT',)])
            if smp_in is not None:
                P.barrier()
            ck('r4')
            stats_rs(oT, cts, lambda kc, ci: ('oT',), nvc, 1.0 / dv)
            for ci, (c0, n) in enumerate(cts):
                for vc in range(nvc):
                    t = ntmp()
                    tt(tmp[t][:, 0:n], oT[:, vc, c0:c0 + n], rs[:, c0:c0 + n], ALU.mult, r=[('oT',), ('rs', ci)], w=[('tmp', t)])
                    stt(Y[:, ybase + vc, c0:c0 + n], tmp[t][:, 0:n], gn_ap(vc), sg[:, vc, c0:c0 + n], ALU.mult, ALU.mult,
                        r=[('tmp', t), ('sg',), ('const',)], w=[('Y', ybase + vc, ci)])
            cnt['nb'] = 8

        for body in range(nbody):
            cts = col_tiles(body)
            nct = len(cts)
            ncb = cts[-1][0] + cts[-1][1]
            P.barrier()
            xin = [carve(arC, i * D * 4, 128, [D], F32) for i in range(2)]
            for blk in range(NPB // 128):
                q = blk % 2
                dma('sp', [(xin[q][:, :], xp[body * NPB + blk * 128: body * NPB + (blk + 1) * 128, :])], w=[('xin', q)])
                for k4 in range(0, KC, 4):
                    b = nps()
                    for j in range(4):
                        kc = k4 + j
                        tr(psb[b][:, j * 128:(j + 1) * 128], xin[q][:, kc * 128:(kc + 1) * 128], ident[:, :], r=[('xin', q), ('const',)], w=[('ps', b)])
                    cpy(A[:, k4:k4 + 4, blk * 128:(blk + 1) * 128], psb[b][:, :].rearrange("p (a b) -> p a b", b=128),
                        r=[('ps', b)], w=[Akey(kc_, blk // 4) for kc_ in range(k4, k4 + 4)], eng='act' if (k4 // 4) % 2 else 'dve')
            if body == 0:
                dma('sp', [(xin[0][0:NS, :], xsm[:, :])], w=[('xin', 0)])
                b = nps()
                for kc in range(KC):
                    tr(psb[b][:, kc * 16:(kc + 1) * 16], xin[0][0:NS, kc * 128:(kc + 1) * 128], ident[0:NS, 0:NS], r=[('xin', 0), ('const',)], w=[('ps', b)])
                cpy(A[:, :, NPB:NCOL], psb[b][:, 0:KC * 16].rearrange("p (a b) -> p a b", b=16), r=[('ps', b)], w=[Akey(kc_, 2) for kc_ in range(KC)])
            P.barrier()

            for l in range(depth):
                dma('sp', [(mod_t[:, :, :], mod_d[l].rearrange("p (a b) -> p a b", b=17))], r=[('modd', l)], w=[('mod',)])
                gpre1 = gv_t[:, 0, l, :].unsqueeze(2).to_broadcast([128, KC, 17])
                gpost1 = gv_t[:, 1, l, :].unsqueeze(2).to_broadcast([128, KC, 17])
                gpre2 = gv_t[:, 2, l, :].unsqueeze(2).to_broadcast([128, KC, 17])
                gpost2 = gv_t[:, 3, l, :].unsqueeze(2).to_broadcast([128, KC, 17])
                stt(gs1[:, :, :], mod_t[:, 16:32, :], 1.0, gpre1, ALU.add, ALU.mult, r=[('mod',), ('const',)], w=[('gsh',)])
                tt(gg1[:, :, :], mod_t[:, 32:48, :], gpost1, ALU.mult, r=[('mod',), ('const',)], w=[('gsh',)])
                stt(gs2[:, :, :], mod_t[:, 64:80, :], 1.0, gpre2, ALU.add, ALU.mult, r=[('mod',), ('const',)], w=[('gsh',)])
                tt(gg2[:, :, :], mod_t[:, 80:96, :], gpost2, ALU.mult, r=[('mod',), ('const',)], w=[('gsh',)])
                ck('xload')
                store_x(cts)
                norm_to_h(cts, gs1, None, 0)
                P.barrier()
                ck('norm1')
                alC.reset(); alAt.reset()
                vt = alC.get(128, [8, BW], BF16)
                lnt = alC.get(128, [BW], F32)
                vts = alC.get(16, [BW], F32)
                vtsb = alC.get(16, [BW], BF16)
                bst = alC.get(128, [16], F32)
                mv = alC.get(128, [4], F32)
                wstf = alC.get(128, [8, 128], F32)
                wstb = alC.get(128, [8, 128], BF16)
                dgb = alC.get(16, [8, 16], BF16)
                LG = alAt.get(128, [BW], F32)
                LB = alAt.get(128, [BW], F32)
                BS = alAt.get(128, [8, 128], F32)
                W00 = alAt.get(16, [8], F32)
                dma('sp', [(LG[:, :], lng[l]), (LB[:, :], lnb[l]), (BS[:, :, :], bsr[l]), (wstf[:, :, :], wst[l]), (W00[:, :], w00[l])], w=[('sguc',)])
                tt(wstb[:, :, :], wstf[:, :, :], tri[:, :].unsqueeze(1).to_broadcast([128, 8, 128]), ALU.mult, r=[('sguc',), ('const',)], w=[('wstb',)])
                tt(dgb[:, :, :], ident[0:16, 0:16].unsqueeze(1).to_broadcast([16, 8, 16]), W00[:, :].unsqueeze(2).to_broadcast([16, 8, 16]), ALU.mult,
                   r=[('sguc',), ('const',)], w=[('dgb',)])
                for u in range(4):
                    def ev(mc, ci, c0, n, b, u=u):
                        gelu_evac(Y[:, u * 2 + mc, c0:c0 + n], psb[b][:, 0:n], 128, n, b, [('Y', u * 2 + mc, ci)])
                    proj(Bh, Bkey, KC, w_in[l, :, O_U + u * 256: O_U + (u + 1) * 256], 256, cts, ev)
                blocks = [(i * 128, 128) for i in range(NPB // 128)] + ([(NPB, NS)] if body == 0 else [])
                for u in range(4):
                    wt, wk = wload(w_in[l, :, O_V + u * 256: O_V + (u + 1) * 256], KC, 256)
                    for bi, (t0, m) in enumerate(blocks):
                        b = nps()
                        for kc in range(KC):
                            mm(psb[b][0:m, 0:256], Bh[:, kc, t0:t0 + m], wt[:, kc, 0:256], kc == 0, kc == KC - 1,
                               r=[wk, Bkey(kc, min(t0 // 512, 2))], w=[('ps', b)])
                        dst = vt[:, bi, u * 256:(u + 1) * 256] if m == 128 else vts[:, u * 256:(u + 1) * 256]
                        gelu_evac(dst, psb[b][0:m, 0:256], m, 256, b, [('vt', bi)])
                for bi, (t0, m) in enumerate(blocks):
                    src = vt[:, bi, :] if m == 128 else vts[:, :]
                    st6 = bst[0:m, 0:12].rearrange("p (a b) -> p a b", b=6)
                    for hh in range(2):
                        P.op('dve', lambda h, o=st6[:, hh, :], i=src[:, hh * 512:(hh + 1) * 512]: h.bn_stats(out=o, in_=i), r=[('vt', bi)], w=[('bst',)])
                    P.op('dve', lambda h, o=mv[0:m, 0:2], i=bst[0:m, 0:12]: h.bn_aggr(out=o, in_=i), r=[('bst',)], w=[('mv',)])
                    act(mv[0:m, 2:3], mv[0:m, 1:2], AF.Sqrt, r=[('mv',)], w=[('mv2',)], bias=EPS)
                    recip(mv[0:m, 2:3], mv[0:m, 2:3], r=[('mv2',)], w=[('mv2',)])
                    ts(lnt[0:m, :], src, mv[0:m, 0:1], mv[0:m, 2:3], ALU.subtract, ALU.mult, r=[('vt', bi), ('mv',), ('mv2',)], w=[('lnt',)])
                    tt(lnt[0:m, :], lnt[0:m, :], LG[0:m, :], ALU.mult, r=[('lnt',), ('sguc',)], w=[('lnt',)])
                    if m == 128:
                        tt(vt[:, bi, :], lnt[:, :], LB[:, :], ALU.add, r=[('lnt',), ('sguc',)], w=[('vt', bi)])
                    else:
                        tt(vts[:, :], lnt[0:m, :], LB[0:m, :], ALU.add, r=[('lnt',), ('sguc',)], w=[('vt', bi)])
                        cpy(vtsb[:, :], vts[:, :], r=[('vt', bi)], w=[('vtsb',)])
                        dma('sp', [(sguv[l], vts[:, :])], r=[('vt', bi)], w=[('sguvout',)])
                for g in range(8):
                    for ci, (c0, n) in enumerate(cts):
                        b = nps()
                        t = ntmp()
                        if n == 512:
                            for j in range(4):
                                bi = c0 // 128 + j
                                mm(psb[b][:, j * 128:(j + 1) * 128], vt[:, bi, g * 128:(g + 1) * 128], wstb[:, g, :], True, True,
                                   r=[('vt', bi), ('wstb',)], w=[('ps', b)])
                            tt(tmp[t][:, :].rearrange("p (a b) -> p a b", b=128), psb[b][:, :].rearrange("p (a b) -> p a b", b=128),
                               BS[:, g, :].unsqueeze(1).to_broadcast([128, 4, 128]), ALU.add, r=[('ps', b), ('sguc',)], w=[('tmp', t)])
                        else:
                            mm(psb[b][:, 0:NS], vtsb[:, g * 128:(g + 1) * 128], dgb[:, g, :], True, True, r=[('vtsb',), ('dgb',)], w=[('ps', b)])
                            tt(tmp[t][:, 0:NS], psb[b][:, 0:NS], BS[:, g, 0:1].to_broadcast([128, NS]), ALU.add, r=[('ps', b), ('sguc',)], w=[('tmp', t)])
                        tt(Y[:, g, c0:c0 + n], tmp[t][:, 0:n], Y[:, g, c0:c0 + n], ALU.mult, r=[('tmp', t), ('Y', g, ci)], w=[('Y', g, ci)])
                P.barrier()
                ck('sgu')
                for hd in range(8):
                    alC.reset(); alAt.reset()
                    la = alC.get(128, [NCOL], F32)
                    qf = alC.get(128, [NCOL], F32)
                    kf = alC.get(128, [NCOL], F32)
                    vT = alAt.get(128, [1, NCOL], BF16)
                    sg = alAt.get(128, [1, NCOL], BF16)
                    def ev_q(mc, ci, c0, n, b):
                        act(qf[:, c0:c0 + n], psb[b][:, 0:n], AF.Silu, r=[('ps', b)], w=[('qf',)])
                    proj(Bh, Bkey, KC, w_in[l, :, O_HQ + hd * 128: O_HQ + (hd + 1) * 128], 128, cts, ev_q)
                    def ev_f(mc, ci, c0, n, b):
                        act(kf[:, c0:c0 + n], psb[b][:, 0:n], AF.Sigmoid, r=[('ps', b)], w=[('kf',)])
                    proj(Bh, Bkey, KC, w_in[l, :, O_HF + hd * 128: O_HF + (hd + 1) * 128], 128, cts, ev_f)
                    act(la[:, 0:ncb], kf[:, 0:ncb], AF.Ln, r=[('kf',), ('oml',), ('lb',)], w=[('la',)],
                        bias=lb_t[:, hd, l:l + 1], scale=oml_t[:, hd, l:l + 1])
                    ts(kf[:, 0:ncb], kf[:, 0:ncb], noml_t[:, hd, l:l + 1], oml_t[:, hd, l:l + 1], ALU.mult, ALU.add,
                       r=[('kf',), ('la',), ('oml',), ('noml',)], w=[('kf',)])
                    def mid_h(hd=hd, vT=vT, sg=sg):
                        def ev_v(mc, ci, c0, n, b):
                            cpy(vT[:, 0, c0:c0 + n], psb[b][:, 0:n], r=[('ps', b)], w=[('vT',)], eng='act')
                        proj(Bh, Bkey, KC, w_in[l, :, O_HI + hd * 128: O_HI + (hd + 1) * 128], 128, cts, ev_v)
                        def ev_g(mc, ci, c0, n, b):
                            act(sg[:, 0, c0:c0 + n], psb[b][:, 0:n], AF.Silu, r=[('ps', b)], w=[('sg',)])
                        proj(Bh, Bkey, KC, w_in[l, :, O_HG + hd * 128: O_HG + (hd + 1) * 128], 128, cts, ev_g)
                    ck('hproj')
                    recur(body, l, cts, qf, kf, la, vT, sg, 1, 32, 1.0, 1.0, lambda vc: hgn_t[:, l:l + 1], 8 + hd,
                          None if body == 0 else shg_d[l, hd],
                          shg_d[l, hd] if body < nbody - 1 else None,
                          hgp[l, hd] if body == nbody - 1 else None,
                          sthg[l, :, hd] if body == 0 else None,
                          hgs[l, :, hd] if body == 0 else None, alC, alAt, mid=mid_h)
                    P.barrier()
                ck('hgrn')
                dma('sp', [(wup_t[:, :], gwup[l])], w=[('wup',)])
                def ev_glr(mc, ci, c0, n, b):
                    cpy(glrT[:, c0:c0 + n], psb[b][0:16, 0:n], r=[('ps', b)], w=[('glr',)])
                proj(Bh, Bkey, KC, w_in[l, :, O_GLR: O_GLR + 16], 16, cts, ev_glr)
                for hd in range(4):
                    alC.reset(); alAt.reset()
                    la = alC.get(128, [NCOL], F32)
                    qf = alC.get(128, [NCOL], F32)
                    kf = alC.get(128, [NCOL], F32)
                    vT = alAt.get(128, [2, NCOL], BF16)
                    sg = alAt.get(128, [2, NCOL], BF16)
                    def ev_q(mc, ci, c0, n, b):
                        cpy(qf[:, c0:c0 + n], psb[b][:, 0:n], r=[('ps', b)], w=[('qf',)], eng='act')
                    proj(Bh, Bkey, KC, w_in[l, :, O_GQ + hd * 128: O_GQ + (hd + 1) * 128], 128, cts, ev_q)
                    def ev_k(mc, ci, c0, n, b):
                        cpy(kf[:, c0:c0 + n], psb[b][:, 0:n], r=[('ps', b)], w=[('kf',)], eng='act')
                    proj(Bh, Bkey, KC, w_in[l, :, O_GK + hd * 128: O_GK + (hd + 1) * 128], 128, cts, ev_k)
                    for ci, (c0, n) in enumerate(cts):
                        b = nps()
                        t = ntmp()
                        mm(psb[b][:, 0:n], wup_t[:, hd * 128:(hd + 1) * 128], glrT[:, c0:c0 + n], True, True, r=[('wup',), ('glr',)], w=[('ps', b)])
                        act(la[:, c0:c0 + n], psb[b][:, 0:n], AF.Sigmoid, r=[('ps', b), ('const',)], w=[('la',)], bias=gbup_t[:, l, hd:hd + 1])
                    act(la[:, 0:ncb], la[:, 0:ncb], AF.Ln, r=[('la',)], w=[('la',)])
                    def mid_g(hd=hd, vT=vT, sg=sg):
                        def ev_v(mc, ci, c0, n, b):
                            cpy(vT[:, mc, c0:c0 + n], psb[b][:, 0:n], r=[('ps', b)], w=[('vT',)], eng='act')
                        proj(Bh, Bkey, KC, w_in[l, :, O_GV + hd * 256: O_GV + (hd + 1) * 256], 256, cts, ev_v)
                        def ev_g(mc, ci, c0, n, b):
                            act(sg[:, mc, c0:c0 + n], psb[b][:, 0:n], AF.Silu, r=[('ps', b)], w=[('sg',)])
                        proj(Bh, Bkey, KC, w_in[l, :, O_GR + hd * 256: O_GR + (hd + 1) * 256], 256, cts, ev_g)
                    recur(body, l, cts, qf, kf, la, vT, sg, 2, 128, 1.0 / 16.0, 128 ** -0.5, lambda vc: gln_t[:, l, vc:vc + 1], 16 + hd * 2,
                          None if body == 0 else sgl_d[l, hd],
                          sgl_d[l, hd] if body < nbody - 1 else None,
                          glp[l, hd] if body == nbody - 1 else None,
                          stgl[l, :, hd] if body == 0 else None,
                          gls[l, :, hd] if body == 0 else None, alC, alAt, mid=mid_g)
                    P.barrier()
                ck('gla')
                def Ykey(kc, ci):
                    return ('Y', kc, ci)
                for m2 in range(8):
                    for br in range(3):
                        wtg, wkg = wload(w_in[l, :, O_GATE + br * D + m2 * 256: O_GATE + br * D + (m2 + 1) * 256], KC, 256)
                        wtb, wkb = wload(w_br[l, br, :, m2 * 256:(m2 + 1) * 256], 8, 256)
                        for mc in range(2):
                            for ci, (c0, n) in enumerate(cts):
                                bg, bp = nps(), nps()
                                for kc in range(KC):
                                    mm(psb[bg][:, 0:n], wtg[:, kc, mc * 128:(mc + 1) * 128], Bh[:, kc, c0:c0 + n], kc == 0, kc == KC - 1,
                                       r=[wkg, Bkey(kc, ci)], w=[('ps', bg)])
                                for kc in range(8):
                                    mm(psb[bp][:, 0:n], wtb[:, kc, mc * 128:(mc + 1) * 128], Y[:, br * 8 + kc, c0:c0 + n], kc == 0, kc == 7,
                                       r=[wkb, Ykey(br * 8 + kc, ci)], w=[('ps', bp)])
                                t = ntmp()
                                act(tmp[t][:, 0:n], psb[bg][:, 0:n], AF.Sigmoid, r=[('ps', bg)], w=[('tmp', t)])
                                if br == 0:
                                    tt(acc[:, mc, c0:c0 + n], tmp[t][:, 0:n], psb[bp][:, 0:n], ALU.mult, r=[('tmp', t), ('ps', bp)], w=[('acc', mc, ci)])
                                else:
                                    tt(tmp[t][:, 0:n], tmp[t][:, 0:n], psb[bp][:, 0:n], ALU.mult, r=[('tmp', t), ('ps', bp)], w=[('tmp', t)])
                                    if br == 1:
                                        tt(acc[:, mc, c0:c0 + n], acc[:, mc, c0:c0 + n], tmp[t][:, 0:n], ALU.add, r=[('tmp', t), ('acc', mc, ci)], w=[('acc', mc, ci)])
                                    else:
                                        tt(Cm[:, m2 * 2 + mc, c0:c0 + n], acc[:, mc, c0:c0 + n], tmp[t][:, 0:n], ALU.add,
                                           r=[('tmp', t), ('acc', mc, ci)], w=[('C', m2 * 2 + mc, ci)])
                P.barrier()
                ck('merge')
                def Ckey(kc, ci):
                    return ('C', kc, ci)
                for u in range(8):
                    def ev(mc, ci, c0, n, b, u=u):
                        cpy(A[:, u * 2 + mc, c0:c0 + n], psb[b][:, 0:n], r=[('ps', b)], w=[Akey(u * 2 + mc, ci)], eng='act' if (mc + ci) % 2 else 'dve')
                    proj(Cm, Ckey, KC, w_out[l, :, u * 256:(u + 1) * 256], 256, cts, ev)
                resid_update(cts, gg1)
                ck('mixer')
                store_x(cts)
                norm_to_h(cts, gs2, None, 48)
                P.barrier()
                for gi in range(4):
                    for u in range(8):
                        def ev(mc, ci, c0, n, b, u=u):
                            t = ntmp()
                            act(tmp[t][:, 0:n], psb[b][:, 0:n], AF.Relu, r=[('ps', b)], w=[('tmp', t)])
                            tt(Cm[:, u * 2 + mc, c0:c0 + n], tmp[t][:, 0:n], tmp[t][:, 0:n], ALU.mult, r=[('tmp', t)], w=[('C', u * 2 + mc, ci)])
                        proj(Bh, Bkey, KC, w_up[l, :, gi * D + u * 256: gi * D + (u + 1) * 256], 256, cts, ev)
                    for u in range(8):
                        def ev(mc, ci, c0, n, b, u=u, gi=gi):
                            if gi == 0:
                                cpy(A[:, u * 2 + mc, c0:c0 + n], psb[b][:, 0:n], r=[('ps', b)], w=[Akey(u * 2 + mc, ci)], eng='act')
                            else:
                                tt(A[:, u * 2 + mc, c0:c0 + n], A[:, u * 2 + mc, c0:c0 + n], psb[b][:, 0:n], ALU.add,
                                   r=[('ps', b), Akey(u * 2 + mc, ci)], w=[Akey(u * 2 + mc, ci)])
                        proj(Cm, Ckey, KC, w_dn[l, gi * D:(gi + 1) * D, u * 256:(u + 1) * 256], 256, cts, ev)
                resid_update(cts, gg2)
                P.barrier()
            yo = [carve(arC, i * D * 4, 128, [D], F32) for i in range(2)]
            for blk in range(NPB // 128):
                q = blk % 2
                for k4 in range(0, KC, 4):
                    b = nps()
                    for j in range(4):
                        kc = k4 + j
                        tr(psb[b][:, j * 128:(j + 1) * 128], A[:, kc, blk * 128:(blk + 1) * 128], ident[:, :], r=[Akey(kc, blk // 4), ('const',)], w=[('ps', b)])
                    cpy(yo[q][:, k4 * 128:(k4 + 4) * 128], psb[b][:, :], r=[('ps', b)], w=[('yo', q)], eng='act' if (k4 // 4) % 2 else 'dve')
                dma('sp', [(yp[body * NPB + blk * 128: body * NPB + (blk + 1) * 128, :], yo[q][:, :])], r=[('yo', q)], w=[('ypout',)])
            if body == 0:
                for k4 in range(0, KC, 4):
                    b = nps()
                    for j in range(4):
                        kc = k4 + j
                        tr(psb[b][0:NS, j * 128:(j + 1) * 128], A[:, kc, NPB:NCOL], ident[:, :], r=[Akey(kc, 2), ('const',)], w=[('ps', b)])
                    cpy(yo[0][0:NS, k4 * 128:(k4 + 4) * 128], psb[b][0:NS, :], r=[('ps', b)], w=[('yo', 0)])
                dma('sp', [(ysm[:, :], yo[0][0:NS, :])], r=[('yo', 0)], w=[('ysmout',)])
            P.barrier()


    try:
        gen()
    except StopBuild:
        pass
    P.barrier()
    P.op('sp', lambda h: None, r=(), w=[('end',)])

    sems = {e: es.enter_context(nc.semaphore("s_" + e)) for e in ENG}
    dsems = {e: [es.enter_context(nc.semaphore("d_%s%d" % (e, i))) for i in range(NDSEM)] for e in ('sp', 'pool')}
    for e in ENG:
        dsems.setdefault(e, [None] * NDSEM)
    P.finalize(sems, dsems)
    with nc.Block() as block:
        @block.tensor
        def _(h):
            P.emit('pe', h, sems)

        @block.scalar
        def _(h):
            P.emit('act', h, sems)

        @block.vector
        def _(h):
            P.emit('dve', h, sems)

        @block.gpsimd
        def _(h):
            P.emit('pool', h, sems)

        @block.sync
        def _(h):
            P.emit('sp', h, sems)
    es.close()
    return nc, {e: len(P.ops[e]) for e in ENG}


def kernel(x_prompt, x_sample, state_hgrn, state_gla, c_prompt, c_sample, w_ada, b_ada, g_pre_mix, g_post_mix,
           g_pre_mlp, g_post_mlp, w_in, sgu_ln_g, sgu_ln_b, sgu_w_s, sgu_b_s, hg_lb, hg_norm_g, gla_w_up, gla_b_up,
           gla_norm_g, w_branch, w_out, w_mlp_up, w_mlp_down, _depth=L_FULL, _stop=None, _ncores=8, _trace=False):
    f = np.float32
    asc = lambda a: np.ascontiguousarray(np.asarray(a, dtype=f))
    Lq = L_FULL
    def fm_vec(v):
        return asc(np.asarray(v).reshape(Lq, KC, 128).transpose(2, 0, 1))
    gvec = asc(np.stack([fm_vec(g_pre_mix), fm_vec(g_post_mix), fm_vec(g_pre_mlp), fm_vec(g_post_mlp)], axis=1))
    bada = asc(np.asarray(b_ada).reshape(Lq, 96, 128).transpose(2, 0, 1))
    lng = asc(np.broadcast_to(np.asarray(sgu_ln_g)[:, None, :], (Lq, 128, BW)))
    lnb = asc(np.broadcast_to(np.asarray(sgu_ln_b)[:, None, :], (Lq, 128, BW)))
    wst = asc(np.asarray(sgu_w_s).transpose(0, 3, 1, 2))
    bsr = asc(np.broadcast_to(np.asarray(sgu_b_s)[:, None, :, :], (Lq, 128, 8, 128)))
    w00 = asc(np.broadcast_to(np.asarray(sgu_w_s)[:, None, :, 0, 0], (Lq, 16, 8)))
    hglb = asc(np.asarray(hg_lb).reshape(Lq, 8, 128).transpose(2, 1, 0))
    hgn = asc(np.asarray(hg_norm_g).T)
    gln = asc(np.asarray(gla_norm_g).reshape(Lq, 2, 128).transpose(2, 0, 1))
    gbup = asc(np.asarray(gla_b_up).reshape(Lq, 4, 128).transpose(2, 0, 1))
    c_ident = np.eye(128, dtype=f)
    c_tri = np.triu(np.ones((128, 128), dtype=f))
    dd = _depth
    shared = dict(w_ada=asc(w_ada[:dd]), w_in=asc(w_in[:dd]), w_br=asc(w_branch[:dd]), w_out=asc(w_out[:dd]), w_up=asc(w_mlp_up[:dd]),
                  w_dn=asc(w_mlp_down[:dd]), bada=bada, gvec=gvec, lng=lng, lnb=lnb, wst=wst, bsr=bsr, w00=w00, hglb=hglb,
                  hgn=hgn, gln=gln, gwup=asc(gla_w_up), gbup=gbup, c_ident=c_ident, c_tri=c_tri)
    xpn, xsn = np.asarray(x_prompt), np.asarray(x_sample)
    shn, sgn = np.asarray(state_hgrn), np.asarray(state_gla)
    cpn, csn = np.asarray(c_prompt), np.asarray(c_sample)
    in_maps = []
    for c in range(8):
        b = c % 4
        crows = np.concatenate([cpn[b:b + 1], csn[c * NS:(c + 1) * NS]], axis=0)
        cfm = asc(crows.reshape(17, KC, 128).transpose(2, 1, 0))
        m = dict(shared)
        m.update(xp=asc(xpn[b]), xsm=asc(xsn[c * NS:(c + 1) * NS, 0, :]), cfm=cfm,
                 sthg=asc(shn[:dd, c * NS:(c + 1) * NS]), stgl=asc(sgn[:dd, c * NS:(c + 1) * NS]))
        in_maps.append(m)
    nc, _ = build(depth=_depth, stop=_stop)
    if _trace:
        res = run_bass_kernel_spmd(nc, in_maps[:_ncores], core_ids=list(range(_ncores)), trace=True)
        return res.exec_time_ns
    res = run_bass_kernel_spmd(nc, in_maps[:_ncores], core_ids=list(range(_ncores)))
    R = res.results
    if _ncores < 8:
        return R
    y_prompt = np.stack([R[b]["yp"] for b in range(4)], axis=0).astype(f)
    y_sample = np.concatenate([R[c]["ysm"] for c in range(8)], axis=0).reshape(128, 1, D).astype(f)
    hgrn_prompt = np.stack([R[b]["hgp"] for b in range(4)], axis=1).astype(f)
    gla_prompt = np.stack([R[b]["glp"] for b in range(4)], axis=1).astype(f)
    hgrn_sample = np.concatenate([R[c]["hgs"] for c in range(8)], axis=1).astype(f)
    gla_sample = np.concatenate([R[c]["gls"] for c in range(8)], axis=1).astype(f)
    sgu_v = np.concatenate([R[c]["sguv"] for c in range(8)], axis=1).reshape(Lq, 128, 1, BW).astype(f)
    return (y_prompt, y_sample, hgrn_prompt, gla_prompt, hgrn_sample, gla_sample, sgu_v)
```
